# Optimizing a Trainium2 kernel written in Bass

```python
import jax, jax.numpy as jnp
from jax import lax
import numpy as np

D_MODEL = 1024
BATCH = 1
SEQ = 16384
DEPTH = 4

MIX_WIDTH = D_MODEL
POOL_WIDTH = D_MODEL // 4
SCONV_WIDTH = 3 * D_MODEL // 8
CCONV_WIDTH = MIX_WIDTH - POOL_WIDTH - SCONV_WIDTH
HEAD_DIM = 64
POOL_WINDOWS = (2, 4, 8, 16)
N_POOL_GROUPS = len(POOL_WINDOWS)
POOL_GROUP = POOL_WIDTH // N_POOL_GROUPS
SCONV_K = 3
CCONV_K = 31
IN_COLS = POOL_WIDTH + 3 * SCONV_WIDTH + 2 * CCONV_WIDTH
D_FF = 2816
LN_EPS = 1e-5
DEEPNORM_ALPHA = (2.0 * DEPTH) ** 0.25
DEEPNORM_BETA = (8.0 * DEPTH) ** -0.25

kernel_name = "hybrid_pool_conv_conformer_encoder"


def layer_norm(x, g, b):
    xf = x.astype(jnp.float32)
    mu = jnp.mean(xf, axis=-1, keepdims=True)
    var = jnp.mean(jnp.square(xf - mu), axis=-1, keepdims=True)
    y = (xf - mu) * lax.rsqrt(var + LN_EPS)
    return (y * g.astype(jnp.float32) + b.astype(jnp.float32)).astype(x.dtype)


def swiglu_ffn(x, w_gate, w_up, w_down):
    return (jax.nn.silu(x @ w_gate) * (x @ w_up)) @ w_down


def depthwise_conv(u, w):
    k = w.shape[0]
    return lax.conv_general_dilated(
        u, w[:, None, :], window_strides=(1,), padding=[(k // 2, k // 2)],
        dimension_numbers=("NWC", "WIO", "NWC"), feature_group_count=u.shape[-1])


def centred_pool_minus_self(u, window):
    seq = u.shape[1]
    uf = u.astype(jnp.float32)
    csum = jnp.pad(jnp.cumsum(uf, axis=1), ((0, 0), (1, 0), (0, 0)))
    t = jnp.arange(seq)
    left = window // 2
    lo = jnp.clip(t - left, 0, seq)
    hi = jnp.clip(t - left + window, 0, seq)
    total = jnp.take(csum, hi, axis=1) - jnp.take(csum, lo, axis=1)
    count = (hi - lo).astype(jnp.float32)
    return (total / count[None, :, None] - uf).astype(u.dtype)


def hybrid_mixer(h, w_in, pool_w, pool_scale, sconv_w, cconv_w, cconv_b, cnorm_g, cnorm_b, w_out):
    bsz, seq = h.shape[0], h.shape[1]
    proj = h @ w_in
    cuts = [POOL_WIDTH,
            POOL_WIDTH + SCONV_WIDTH,
            POOL_WIDTH + 2 * SCONV_WIDTH,
            POOL_WIDTH + 3 * SCONV_WIDTH,
            POOL_WIDTH + 3 * SCONV_WIDTH + CCONV_WIDTH]
    u_pool, gate_b, gate_c, v, c_val, c_gate = jnp.split(proj, cuts, axis=-1)

    pooled = jnp.stack(
        [centred_pool_minus_self(u_pool[..., g * POOL_GROUP:(g + 1) * POOL_GROUP], w)
         for g, w in enumerate(POOL_WINDOWS)], axis=2)
    y_a = jnp.einsum("bsgc,gcd->bsgd", pooled, pool_w).reshape(bsz, seq, POOL_WIDTH) * pool_scale

    y_b = gate_b * depthwise_conv(gate_c * v, sconv_w)

    a = c_val * jax.nn.sigmoid(c_gate)
    a = depthwise_conv(a, cconv_w) + cconv_b
    y_c = jax.nn.silu(layer_norm(a, cnorm_g, cnorm_b))

    return jnp.concatenate([y_a, y_b, y_c], axis=-1) @ w_out


def setup_inputs(seed: int = 0) -> dict:
    key = jax.random.key(seed)
    ks = jax.random.split(key, 22)

    def nrm(k, shape, scale):
        return jax.random.normal(k, shape, jnp.float32) * scale

    d, f = D_MODEL, D_FF
    return {
        "x": nrm(ks[0], (BATCH, SEQ, d), 1.0),
        "ln1_g": 1.0 + nrm(ks[1], (DEPTH, d), 0.05),
        "ln1_b": nrm(ks[2], (DEPTH, d), 0.02),
        "ffn1_w_gate": nrm(ks[3], (DEPTH, d, f), d ** -0.5),
        "ffn1_w_up": nrm(ks[4], (DEPTH, d, f), d ** -0.5),
        "ffn1_w_down": nrm(ks[5], (DEPTH, f, d), DEEPNORM_BETA * f ** -0.5),
        "mix_w_in": nrm(ks[6], (DEPTH, d, IN_COLS), d ** -0.5),
        "pool_w": nrm(ks[7], (DEPTH, N_POOL_GROUPS, POOL_GROUP, POOL_GROUP), POOL_GROUP ** -0.5),
        "pool_scale": 1.0 + nrm(ks[8], (DEPTH, POOL_WIDTH), 0.05),
        "sconv_w": nrm(ks[9], (DEPTH, SCONV_K, SCONV_WIDTH), SCONV_K ** -0.5),
        "cconv_w": nrm(ks[10], (DEPTH, CCONV_K, CCONV_WIDTH), CCONV_K ** -0.5),
        "cconv_b": nrm(ks[11], (DEPTH, CCONV_WIDTH), 0.02),
        "cnorm_g": 1.0 + nrm(ks[12], (DEPTH, CCONV_WIDTH), 0.05),
        "cnorm_b": nrm(ks[13], (DEPTH, CCONV_WIDTH), 0.02),
        "mix_w_out": nrm(ks[14], (DEPTH, MIX_WIDTH, d), DEEPNORM_BETA * MIX_WIDTH ** -0.5),
        "ln2_g": 1.0 + nrm(ks[15], (DEPTH, d), 0.05),
        "ln2_b": nrm(ks[16], (DEPTH, d), 0.02),
        "ffn2_w_gate": nrm(ks[17], (DEPTH, d, f), d ** -0.5),
        "ffn2_w_up": nrm(ks[18], (DEPTH, d, f), d ** -0.5),
        "ffn2_w_down": nrm(ks[19], (DEPTH, f, d), DEEPNORM_BETA * f ** -0.5),
        "ln3_g": 1.0 + nrm(ks[20], (DEPTH, d), 0.05),
        "ln3_b": nrm(ks[21], (DEPTH, d), 0.02),
    }


def reference(x, ln1_g, ln1_b, ffn1_w_gate, ffn1_w_up, ffn1_w_down, mix_w_in, pool_w,
              pool_scale, sconv_w, cconv_w, cconv_b, cnorm_g, cnorm_b, mix_w_out,
              ln2_g, ln2_b, ffn2_w_gate, ffn2_w_up, ffn2_w_down, ln3_g, ln3_b):
    for l in range(DEPTH):
        x = layer_norm(DEEPNORM_ALPHA * x
                       + 0.5 * swiglu_ffn(x, ffn1_w_gate[l], ffn1_w_up[l], ffn1_w_down[l]),
                       ln1_g[l], ln1_b[l])
        x = layer_norm(DEEPNORM_ALPHA * x
                       + hybrid_mixer(x, mix_w_in[l], pool_w[l], pool_scale[l], sconv_w[l],
                                      cconv_w[l], cconv_b[l], cnorm_g[l], cnorm_b[l], mix_w_out[l]),
                       ln2_g[l], ln2_b[l])
        x = layer_norm(DEEPNORM_ALPHA * x
                       + 0.5 * swiglu_ffn(x, ffn2_w_gate[l], ffn2_w_up[l], ffn2_w_down[l]),
                       ln3_g[l], ln3_b[l])
    return x
```

```python
import numpy as np
import concourse.bass as bass
import concourse.mybir as mybir
from concourse.bass_utils import run_bass_kernel_spmd

F32 = mybir.dt.float32
BF16 = mybir.dt.bfloat16
AF = mybir.ActivationFunctionType
ALU = mybir.AluOpType

P = 128
DEPTH = 4
D = 1024
DFF = 2816
NJ = DFF // P
NKC = D // P
SEQ = 16384
NCORES = 8
NSTREAM = 2
OWN = SEQ // (NCORES * NSTREAM)
HALO = 60
T = OWN + 2 * HALO
TILES = [(0, 384), (384, 384), (768, 376)]
PAD = 16
TP = T + 2 * PAD
ALPHA = (2.0 * DEPTH) ** 0.25
EPS = 1e-5
NV = 161
NSF = 5
NSM = 3
SLOT = 1024
GROUPS = [(0, 8), (8, 7), (15, 7)]
GMAX = 8
BG_BOOST = 1.0
WIN_PERM = [14, 11, 15, 12, 16, 13, 0, 1, 5, 8, 2, 6, 9, 3, 7, 10, 4]
V_LN = [(0, 8), (16, 24), (32, 40)]
V_PSCALE = 48
V_SCONV = 50
V_CCONV = 59
V_CB = 152
V_CNG = 155
V_CNB = 158
EL = HALO - 16
ER = HALO + OWN - 16

INTERLEAVE = True


def tiles_for(h):
    lo = HALO - h
    W = OWN + 2 * h
    units = W // 2
    base, rem = divmod(units, 3)
    out = []
    c = lo
    for i in range(3):
        n = 2 * (base + (1 if i < rem else 0))
        out.append((c, n))
        c += n
    return out


def halo_even(h):
    return h + (h % 2)
FUSED = True


class Ev:
    __slots__ = ("sem", "val")

    def __init__(self, sem, val):
        self.sem = sem
        self.val = val


class Slot:
    __slots__ = ("ap", "free")

    def __init__(self, ap):
        self.ap = ap
        self.free = None


class Rot:
    def __init__(self, slots):
        self.slots = slots
        self.i = 0

    def next(self):
        s = self.slots[self.i % len(self.slots)]
        self.i += 1
        return s


class Prog:
    ENGS = ("pe", "act", "dve", "pool", "sp")

    def __init__(self, nc):
        self.nc = nc
        self.ops = {e: [] for e in self.ENGS}
        self.sems = {}
        self.cnt = {}
        self.nsem = 0

    def sem(self, name):
        if name not in self.sems:
            self.sems[name] = None
            self.cnt[name] = 0
        return name

    def emit(self, eng, fn, waits=(), sem=None, inc=1, signal=True):
        ws = [(w.sem, w.val) for w in waits if w is not None]
        ev = None
        s = None
        if signal:
            s = sem if sem is not None else self.sem("e_" + eng)
            self.sem(s)
            self.cnt[s] += inc
            ev = Ev(s, self.cnt[s])
        self.ops[eng].append((fn, ws, s, inc))
        return ev

    def replay(self, eng, e):
        waited = {}
        for fn, ws, s, inc in self.ops[eng]:
            for (ws_, wv) in ws:
                if waited.get(ws_, 0) < wv:
                    e.wait_ge(self.sems[ws_], wv)
                    waited[ws_] = wv
            if fn is None:
                continue
            ins = fn(e)
            if s is not None:
                ins.then_inc(self.sems[s], inc)


def build(nl, final_unscaled=True):
    nc = bass.Bass("TRN2", target_bir_lowering=False)

    xin = nc.dram_tensor("xin", [NSTREAM, P, NKC * T], F32, kind="ExternalInput").ap()
    auxd = nc.dram_tensor("aux", [NSTREAM, P, 130], F32, kind="ExternalInput").ap()
    vecd = nc.dram_tensor("vec", [P, nl * NV], F32, kind="ExternalInput").ap()
    wgud = nc.dram_tensor("wgu", [nl * 2 * NJ, P, 2048], F32, kind="ExternalInput").ap()
    wdd = nc.dram_tensor("wd", [nl * 2 * NKC, P, DFF], F32, kind="ExternalInput").ap()
    wind = nc.dram_tensor("win", [nl * 17, P, 1024], F32, kind="ExternalInput").ap()
    woutd = nc.dram_tensor("wout", [nl * NKC, P, 1024], F32, kind="ExternalInput").ap()
    wpoold = nc.dram_tensor("wpool", [nl, P, 256], F32, kind="ExternalInput").ap()
    identd = nc.dram_tensor("ident", [P, P], F32, kind="ExternalInput").ap()
    outd = nc.dram_tensor("out", [NSTREAM, P, NKC * OWN], F32, kind="ExternalOutput").ap()

    A = nc.alloc_sbuf_tensor
    xs = [A(f"xs{s}", [P, NKC, T], F32) for s in range(NSTREAM)]
    xb = [A(f"xb{s}", [P, NKC, T], BF16) for s in range(NSTREAM)]
    HSL = 19
    HM = A("HM", [P, 8 * T + 11712], BF16)
    HF = A("HF", [P, GMAX * T], BF16)
    ringF_t = A("ringF", [P, NSF, SLOT], BF16)
    ringM_t = A("ringM", [P, NSM, SLOT], BF16)
    vecs = A("vecs", [P, nl * NV], F32)
    auxs = [A(f"aux{s}", [P, 130], F32) for s in range(NSTREAM)]
    sgF_t = A("sgF", [P, 2, 384], F32)
    sgB_t = A("sgB", [P, 2, 384], F32)
    sq_t = A("sq", [P, 4, 384], BF16)
    nt_t = A("nt", [P, 2, 384], F32)
    mean_sb = A("mean_sb", [P, T], F32)
    var_sb = A("var_sb", [P, T], F32)
    ones_t = A("ones", [P, P], BF16)
    cst = A("cst", [P, 4], F32)
    ident_f = A("ident_f", [P, P], F32)
    dg_t = A("dg", [P, 2, P], BF16)
    banks = [nc.alloc_psum_tensor(f"bank{i}", [P, 512], F32) for i in range(8)]

    def Yslot(i):
        return HM[:, i * T:(i + 1) * T]

    def Hslot(i):
        return HF[:, i * T:(i + 1) * T]

    TA = HM[:, 8 * T:8 * T + 11712].bitcast(F32)

    def vcol(li, off):
        c = li * NV + off
        return vecs[:, c:c + 1]

    def emit_all(counts_in):
        p = Prog(nc)
        counts = {}
        sgF = Rot([Slot(sgF_t[:, i, :]) for i in range(2)])
        sgB = Rot([Slot(sgB_t[:, i, :]) for i in range(2)])
        sqring = Rot([Slot(sq_t[:, i, :]) for i in range(4)])
        ntring = Rot([Slot(nt_t[:, i, :]) for i in range(2)])
        dgring = Rot([Slot(dg_t[:, i, :]) for i in range(2)])
        bs = [Slot(b) for b in banks]
        bG = Rot([bs[0], bs[1]])
        bU = Rot([bs[2], bs[3]])
        bDn = Rot([bs[0], bs[1], bs[2], bs[3]])
        bP0 = bs[4]
        bP1 = bs[5]
        bS = Rot([bs[4], bs[5]])
        bM = bs[6]
        bQ = bs[7]

        class Ring:
            def __init__(self, tens, ns, name):
                self.t, self.ns, self.name = tens, ns, name
                self.plan = []
                self.issued = 0
                self.consumed = 0
                self.rel = {}
                self.ev = {}

            def issue_upto(self, k):
                while self.issued <= k and self.issued < len(self.plan):
                    m = self.issued
                    src, n = self.plan[m]
                    sl = m % self.ns
                    waits = []
                    if m >= self.ns:
                        if (m - self.ns) not in self.rel:
                            break
                        waits.append(self.rel[m - self.ns])
                    dst = self.t[:, sl, 0:n]
                    self.ev[m] = p.emit(
                        "pool", (lambda e, dst=dst, src=src: e.dma_start(out=dst, in_=src)),
                        waits, sem=p.sem(f"{self.name}{sl}"), inc=16)
                    self.issued += 1

            def next(self):
                k = self.consumed
                self.issue_upto(k + self.ns - 1)
                assert self.issued > k, "ring: load not issuable (missing release)"
                self.consumed += 1
                return k, self.t[:, k % self.ns, :], self.ev[k]

            def release(self, k, ev):
                self.rel[k] = ev

        ringF = Ring(ringF_t, NSF, "rf")
        ringM = Ring(ringM_t, NSM, "rm")

        def ffn_loads(li, f):
            base = (li * 2 + f)
            out = []
            for (j0, nj) in GROUPS:
                for jj in range(nj):
                    row = wgud[base * NJ + j0 + jj]
                    out.append((row[:, 0:1024], 1024))
                    out.append((row[:, 1024:2048], 1024))
                for dc in range(NKC):
                    out.append((wdd[base * NKC + dc][:, j0 * P:(j0 + nj) * P], nj * P))
            return out

        def mix_loads(li):
            b = li * 17
            out = [(wind[b + i], 1024) for i in range(8)]
            out.append((wpoold[li], 256))
            out += [(wind[b + i], 1024) for i in range(8, 17)]
            for dc in range(NKC):
                out.append((woutd[li * NKC + dc], 1024))
            return out

        for li in range(nl):
            for S in range(NSTREAM):
                ringF.plan += ffn_loads(li, 0)
            for S in range(NSTREAM):
                ringF.plan += ffn_loads(li, 1)
            for S in range(NSTREAM):
                ringM.plan += mix_loads(li)

        def mm(out, lhsT, rhs, start, stop, waits=(), signal=False):
            return p.emit("pe", lambda e: e.matmul(out, lhsT, rhs, start=start, stop=stop), waits, signal=signal)

        def act(out, in_, func, scale=1.0, bias=0.0, waits=()):
            return p.emit("act", lambda e: e.activation(out=out, in_=in_, func=func, bias=bias, scale=scale), waits)

        def tt(out, in0, in1, op, waits=()):
            return p.emit("dve", lambda e: e.tensor_tensor(out=out, in0=in0, in1=in1, op=op), waits)

        def stt(out, in0, scalar, in1, op0, op1, waits=()):
            return p.emit("dve", lambda e: e.scalar_tensor_tensor(out=out, in0=in0, scalar=scalar, in1=in1, op0=op0, op1=op1), waits)

        def ts(out, in0, s1, s2, op0, op1, waits=()):
            return p.emit("dve", lambda e: e.tensor_scalar(out=out, in0=in0, scalar1=s1, scalar2=s2, op0=op0, op1=op1), waits)

        def pcopy(out, in_, waits=()):
            return p.emit("pool", lambda e: e.tensor_copy(out=out, in_=in_), waits)

        def memset(out, val, waits=()):
            return p.emit("dve", lambda e: e.memset(out, val), waits)

        xb_ev = [[None] * 3 for _ in range(NSTREAM)]
        xs_ev = [[None] * 3 for _ in range(NSTREAM)]
        st = {"HF_free": [], "HM_free": [], "ln_free": None, "xs_all0": None, "xs_all1": None}

        ld_ev = []
        for s in range(NSTREAM):
            ld_ev.append(p.emit("sp", lambda e, s=s: e.dma_start(out=xs[s][:].rearrange("p a t -> p (a t)"), in_=xin[s]),
                                sem=p.sem(f"ldx{s}"), inc=16))
        aux_ev = [p.emit("sp", lambda e, s=s: e.dma_start(out=auxs[s][:], in_=auxd[s]), sem=p.sem(f"lda{s}"), inc=16)
                  for s in range(NSTREAM)]
        vec_ev = p.emit("sp", lambda e: e.dma_start(out=vecs[:], in_=vecd), sem=p.sem("ldv"), inc=16)
        ident_ev = p.emit("sp", lambda e: e.dma_start(out=ident_f[:], in_=identd), sem=p.sem("ldi"), inc=16)
        memset(ones_t[:], 1.0)
        memset(cst[0:64, 0:1], 0.5)
        memset(cst[64:128, 0:1], 0.25)
        memset(cst[0:64, 1:2], 0.125)
        const_ev = memset(cst[64:128, 1:2], 0.0625)
        for s in range(NSTREAM):
            e1 = None
            for dc in range(NKC):
                e1 = act(xb[s][:, dc, :], xs[s][:, dc, :], AF.Copy, waits=[ld_ev[s]])
            for ti in range(3):
                xb_ev[s][ti] = e1
                xs_ev[s][ti] = ld_ev[s]

        def ln_task(nch, inv_n, src, in_ev, out_fn, tiles):
            first = True
            d1 = a1 = None
            for ti, (c0, n) in enumerate(tiles):
                pe_ev = None

                def stat_mm(c, r1, r2, e1, e2, n=n):
                    m1 = mm(bM.ap[:, :n], ones_t[:], r1.ap[:, :n], c == 0, c == nch - 1,
                            [e1] + ([bM.free] if c == 0 else []), signal=True)
                    m2 = mm(bQ.ap[:, :n], ones_t[:], r2.ap[:, :n], c == 0, c == nch - 1,
                            [e2] + ([bQ.free] if c == 0 else []), signal=True)
                    r1.free = m1
                    r2.free = m2
                    return m2

                pend = None
                for c in range(nch):
                    r1 = sqring.next()
                    w = [in_ev[ti], r1.free]
                    if first:
                        w += [const_ev, vec_ev]
                    e1 = pcopy(r1.ap[:, :n], src(c, c0, n), w)
                    r2 = sqring.next()
                    e2 = act(r2.ap[:, :n], src(c, c0, n), AF.Square, waits=[in_ev[ti], r2.free])
                    if pend is not None:
                        pe_ev = stat_mm(*pend)
                    pend = (c, r1, r2, e1, e2)
                    first = False
                    yield 0.9
                pe_ev = stat_mm(*pend)
                w = [pe_ev]
                if ti == 0 and st["ln_free"] is not None:
                    w.append(st["ln_free"])
                a1 = act(mean_sb[:, c0:c0 + n], bM.ap[:, :n], AF.Identity, scale=inv_n, waits=w)
                nt = ntring.next()
                a2 = act(nt.ap[:, :n], bM.ap[:, :n], AF.Square, scale=inv_n, waits=[nt.free])
                bM.free = a2
                d1 = stt(var_sb[:, c0:c0 + n], bQ.ap[:, :n], inv_n, nt.ap[:, :n], ALU.mult, ALU.subtract,
                         [a2, pe_ev] + ([st["ln_free"]] if ti == 0 else []))
                bQ.free = d1
                nt.free = d1
                yield 1.2
            d2 = ts(var_sb[:], var_sb[:], 0.0, EPS, ALU.max, ALU.add, [d1])
            a3 = act(var_sb[:], var_sb[:], AF.Sqrt, waits=[d2])
            d3 = p.emit("dve", lambda e: e.reciprocal(out=var_sb[:], in_=var_sb[:]), [a3])
            d4 = stt(mean_sb[:], mean_sb[:], -1.0, var_sb[:], ALU.mult, ALU.mult, [d3, a1])
            yield 5.0
            last = None
            for ti, (c0, n) in enumerate(tiles):
                for c in range(nch):
                    nt = ntring.next()
                    e1 = tt(nt.ap[:, :n], src(c, c0, n), var_sb[:, c0:c0 + n], ALU.mult, [d4, nt.free])
                    e2 = tt(nt.ap[:, :n], nt.ap[:, :n], mean_sb[:, c0:c0 + n], ALU.add, [e1])
                    last = e2
                    nt.free = out_fn(c, ti, c0, n, nt.ap[:, :n], e2)
                    yield 0.9
            st["ln_free"] = last

        def main_ln(S, li, which, y_ev, tiles):
            goff, boff = V_LN[which]

            def src(c, c0, n):
                return xs[S][:, c, c0:c0 + n]

            def out_fn(c, ti, c0, n, nt_ap, ev):
                e1 = act(xs[S][:, c, c0:c0 + n], nt_ap, AF.Identity, scale=vcol(li, goff + c), bias=vcol(li, boff + c), waits=[ev])
                e2 = pcopy(xb[S][:, c, c0:c0 + n], xs[S][:, c, c0:c0 + n], [e1])
                xs_ev[S][ti] = e1
                xb_ev[S][ti] = e2
                st["xs_all%d" % S] = e1
                return e1

            return ln_task(NKC, 1.0 / D, src, y_ev, out_fn, tiles)

        def h_sync(key):
            if st[key]:
                p.emit("act", None, st[key], signal=False)
                p.emit("dve", None, st[key], signal=False)
            st[key] = []

        def ffn_phase(S, li, f, y_ev, TILES):
            X = xb[S]
            XS = xs[S]
            h_sync("HF_free")
            hfree = None
            last_down = None
            h_last_dve = None
            for gi, (j0, nj) in enumerate(GROUPS):
                h_ev = [[None] * 3 for _ in range(nj)]
                for jj in range(nj):
                    kg, slot_g, wev_g = ringF.next()
                    ku, slot_u, wev_u = ringF.next()
                    wg = slot_g.rearrange("p (k m) -> p k m", k=NKC)
                    wu = slot_u.rearrange("p (k m) -> p k m", k=NKC)
                    up_ev = gate_ev = None
                    for ti, (c0, n) in enumerate(TILES):
                        g_b = bG.next()
                        u_b = bU.next()
                        for kc in range(NKC):
                            w = [wev_g, xb_ev[S][ti], g_b.free] if kc == 0 else []
                            gate_ev = mm(g_b.ap[:, :n], wg[:, kc, :], X[:, kc, c0:c0 + n], kc == 0, kc == NKC - 1, w, signal=(kc == NKC - 1))
                        for kc in range(NKC):
                            w = [wev_u, u_b.free] if kc == 0 else []
                            up_ev = mm(u_b.ap[:, :n], wu[:, kc, :], X[:, kc, c0:c0 + n], kc == 0, kc == NKC - 1, w, signal=(kc == NKC - 1))
                        sgs = sgF.next()
                        a_ev = act(sgs.ap[:, :n], g_b.ap[:, :n], AF.Silu, waits=[gate_ev, sgs.free])
                        g_b.free = a_ev
                        d_ev = stt(Hslot(jj)[:, c0:c0 + n], sgs.ap[:, :n], 0.5, u_b.ap[:, :n], ALU.mult, ALU.mult, [a_ev, up_ev, hfree])
                        u_b.free = d_ev
                        sgs.free = d_ev
                        h_ev[jj][ti] = d_ev
                        h_last_dve = d_ev
                        yield 2.9
                    ringF.release(kg, gate_ev)
                    ringF.release(ku, up_ev)
                for dc in range(NKC):
                    k, slot, wev = ringF.next()
                    ev = None
                    for ti, (c0, n) in enumerate(TILES):
                        d_b = bDn.next()
                        for kk in range(nj):
                            w = [wev, d_b.free, h_ev[nj - 1][ti]] if kk == 0 else []
                            ev = mm(d_b.ap[:, :n], slot[:, kk * P:(kk + 1) * P], Hslot(kk)[:, c0:c0 + n], kk == 0, kk == nj - 1, w, signal=(kk == nj - 1))
                        if gi == 0:
                            d_ev = stt(XS[:, dc, c0:c0 + n], XS[:, dc, c0:c0 + n], ALPHA, d_b.ap[:, :n], ALU.mult, ALU.add, [ev, xs_ev[S][ti]])
                        else:
                            d_ev = tt(XS[:, dc, c0:c0 + n], XS[:, dc, c0:c0 + n], d_b.ap[:, :n], ALU.add, [ev, y_ev[ti]])
                        d_b.free = d_ev
                        if dc == NKC - 1:
                            y_ev[ti] = d_ev
                        yield 0.18 * nj + 0.1
                    ringF.release(k, ev)
                    last_down = ev
                hfree = last_down
            st["HF_free"] = [last_down, h_last_dve]

        def mixer_phase(S, li, y_ev, TILES_A, TILES_B):
            X = xb[S]
            XS = xs[S]
            AX = auxs[S]
            mL = AX[:, 0:1]
            mR = AX[:, 1:2]
            h_sync("HM_free")
            Y = [Yslot(i) for i in range(8)]

            def wview(slot):
                return slot.rearrange("p (k m) -> p k m", k=NKC)

            def inproj(bank, wv_c, ti, c0, n, extra):
                ev = None
                for kc in range(NKC):
                    w = ([xb_ev[S][ti], bank.free] + extra) if kc == 0 else []
                    ev = mm(bank.ap[:, :n], wv_c[:, kc, :], X[:, kc, c0:c0 + n], kc == 0, kc == NKC - 1, w, signal=(kc == NKC - 1))
                return ev

            ABF = HM[:, 8 * T:8 * T + 3 * TP]
            abuf = [ABF[:, c * TP:(c + 1) * TP] for c in range(3)]
            CO0 = (3 * TP) // 2
            cobuf = [TA[:, CO0 + c * T:CO0 + (c + 1) * T] for c in range(3)]
            a3 = ABF.rearrange("p (c t) -> p c t", c=3)
            memset(a3[:, :, 0:PAD], 0.0)
            pad_ev = memset(a3[:, :, TP - PAD:TP], 0.0)
            a_done = [None] * 3
            for c in range(3):
                k1, slot1, wev1 = ringM.next()
                k2, slot2, wev2 = ringM.next()
                ge = ve = d_ev = None
                for ti, (c0, n) in enumerate(TILES_A):
                    ge = inproj(bP0, wview(slot1), ti, c0, n, [wev1])
                    ve = inproj(bP1, wview(slot2), ti, c0, n, [wev2])
                    sgs = sgB.next()
                    a_ev = act(sgs.ap[:, :n], bP0.ap[:, :n], AF.Tanh, scale=0.5, waits=[ge, sgs.free])
                    bP0.free = a_ev
                    d_ev = stt(abuf[c][:, PAD + c0:PAD + c0 + n], sgs.ap[:, :n], 1.0, bP1.ap[:, :n], ALU.add, ALU.mult, [a_ev, ve, pad_ev])
                    bP1.free = d_ev
                    sgs.free = d_ev
                    yield 2.9
                ringM.release(k1, ge)
                ringM.release(k2, ve)
                ts(abuf[c][:, PAD:PAD + HALO], abuf[c][:, PAD:PAD + HALO], mL, 0.0, ALU.mult, ALU.add, [d_ev, aux_ev[S]])
                a_done[c] = ts(abuf[c][:, PAD + HALO + OWN:PAD + T], abuf[c][:, PAD + HALO + OWN:PAD + T], mR, 0.0, ALU.mult, ALU.add, [d_ev])
            cbanks = [bs[4], bs[5], bs[6]]
            co_ev = [None] * 3
            for c in range(3):
                last_me = [None] * 3

                def build_diag(kk, c=c):
                    dg_ = dgring.next()
                    out_ap, w_ap = dg_.ap, vcol(li, V_CCONV + kk * 3 + c)
                    de_ = p.emit("pool", lambda e: e.tensor_scalar(out=out_ap, in0=ident_f[:], scalar1=w_ap, scalar2=0.0,
                                                                    op0=ALU.mult, op1=ALU.add),
                                 [dg_.free, vec_ev, ident_ev])
                    return dg_, de_

                nxt = build_diag(0)
                for kk in range(31):
                    dg, de = nxt
                    if kk + 1 < 31:
                        nxt = build_diag(kk + 1)
                    me = None
                    for ti, (c0, n) in enumerate(TILES_B):
                        w = [de] + ([a_done[c], cbanks[ti].free] if kk == 0 else [])
                        me = mm(cbanks[ti].ap[:, :n], dg.ap, abuf[c][:, PAD + c0 + kk - 15:PAD + c0 + kk - 15 + n], kk == 0, kk == 30, w,
                                signal=(kk == 30 or ti == 2))
                        if kk == 30:
                            last_me[ti] = me
                    dg.free = me
                    yield 0.6
                for ti, (c0, n) in enumerate(TILES_B):
                    ae = act(cobuf[c][:, c0:c0 + n], cbanks[ti].ap[:, :n], AF.Identity, scale=0.5, bias=vcol(li, V_CB + c), waits=[last_me[ti]])
                    cbanks[ti].free = ae
                    co_ev[ti] = ae
                yield 1.5
            yc_ev = [None] * 3

            def c_src(c, c0, n):
                return cobuf[c][:, c0:c0 + n]

            def c_out(c, ti, c0, n, nt_ap, ev):
                e = act(Y[5 + c][:, c0:c0 + n], nt_ap, AF.Silu, scale=vcol(li, V_CNG + c), bias=vcol(li, V_CNB + c), waits=[ev])
                yc_ev[ti] = e
                return e

            for cst_ in ln_task(3, 1.0 / 384.0, c_src, co_ev, c_out, TILES_B):
                yield cst_
            c_done_dve = st["ln_free"]
            c_done_act = yc_ev[2]

            ub = [TA[:, c * TP:(c + 1) * TP] for c in range(2)]
            s_a = TA[:, 2 * TP:3 * TP]
            s_b = TA[:, 3 * TP:4 * TP]
            tot = TA[:, 4 * TP:4 * TP + T]
            u2 = TA[:, 0:2 * TP].rearrange("p (c t) -> p c t", c=2)
            p.emit("act", None, [c_done_dve], signal=False)
            p.emit("dve", None, [c_done_act], signal=False)
            memset(u2[:, :, 0:PAD], 0.0)
            pad_ev = memset(u2[:, :, TP - PAD:TP], 0.0)
            u_done = [None] * 2
            for c in range(2):
                k, slot, wev = ringM.next()
                a_ev = ue = None
                for ti, (c0, n) in enumerate(TILES_A):
                    d_b = bS.next()
                    ue = inproj(d_b, wview(slot), ti, c0, n, [wev])
                    a_ev = act(ub[c][:, PAD + c0:PAD + c0 + n], d_b.ap[:, :n], AF.Copy, waits=[ue, pad_ev, c_done_dve])
                    d_b.free = a_ev
                    yield 1.5
                ringM.release(k, ue)
                ts(ub[c][:, PAD:PAD + HALO], ub[c][:, PAD:PAD + HALO], mL, 0.0, ALU.mult, ALU.add, [a_ev, aux_ev[S]])
                u_done[c] = ts(ub[c][:, PAD + HALO + OWN:PAD + T], ub[c][:, PAD + HALO + OWN:PAD + T], mR, 0.0, ALU.mult, ALU.add, [a_ev])
            k, slot, wev = ringM.next()
            pw = slot[:, 0:256].rearrange("p (c m) -> p c m", c=2)
            pooled = [Y[2], Y[3]]
            pool_dve = None
            ya_ev = None
            pool_ready = [None, None]
            for c in range(2):
                ev = u_done[c]
                for hf in range(2):
                    r0, r1 = hf * 64, hf * 64 + 64
                    gidx = 2 * c + hf
                    w_ = 2 ** (gidx + 1)
                    srcb = ub[c]
                    bufs = [s_a, s_b]
                    step = 1
                    nb = 0
                    while step * 2 < w_:
                        dst = bufs[nb % 2]
                        L = TP - step
                        ev = tt(dst[r0:r1, 0:L], srcb[r0:r1, 0:L], srcb[r0:r1, step:step + L], ALU.add, [ev])
                        srcb = dst
                        nb += 1
                        step *= 2
                    h2 = w_ // 2
                    ev = tt(tot[r0:r1, :], srcb[r0:r1, PAD - h2:PAD - h2 + T], srcb[r0:r1, PAD:PAD + T], ALU.add, [ev])
                ev = stt(pooled[c][:, :], tot[:, :], cst[:, c:c + 1], ub[c][:, PAD:PAD + T], ALU.mult, ALU.subtract, [ev, const_ev])
                for e_i, e0 in enumerate((EL, ER)):
                    nt = ntring.next()
                    e1 = tt(nt.ap[:, 0:32], tot[:, e0:e0 + 32], AX[:, 2 + c * 64 + e_i * 32:2 + c * 64 + e_i * 32 + 32], ALU.mult, [ev, nt.free])
                    ev = tt(pooled[c][:, e0:e0 + 32], nt.ap[:, 0:32], ub[c][:, PAD + e0:PAD + e0 + 32], ALU.subtract, [e1])
                    nt.free = ev
                yield 8.0
                pool_ready[c] = ev
                pool_dve = ev
            pool_k, pool_wev = k, wev

            def pool_matmuls():
                me = ya = None
                for c in range(2):
                    for ti, (c0, n) in enumerate(TILES_B):
                        d_b = bS.next()
                        me = mm(d_b.ap[:, :n], pw[:, c, :], pooled[c][:, c0:c0 + n], True, True, [pool_wev, pool_ready[c], d_b.free], signal=True)
                        ya = act(Y[c][:, c0:c0 + n], d_b.ap[:, :n], AF.Identity, scale=vcol(li, V_PSCALE + c), waits=[me])
                        d_b.free = ya
                ringM.release(pool_k, me)
                return me, ya

            gcv = [TA[:, c * TP:(c + 1) * TP] for c in range(3)]
            cv = TA[:, 3 * TP:3 * TP + T]
            g3 = TA[:, 0:3 * TP].rearrange("p (c t) -> p c t", c=3)
            memset(g3[:, :, 0:PAD], 0.0, [pool_dve])
            pad_ev = memset(g3[:, :, TP - PAD:TP], 0.0)
            cv_free = None
            yb_last = None
            for c in range(3):
                k1, slot1, wev1 = ringM.next()
                k2, slot2, wev2 = ringM.next()
                ge = ve = d_ev = None
                for ti, (c0, n) in enumerate(TILES_A):
                    ge = inproj(bP0, wview(slot1), ti, c0, n, [wev1])
                    ve = inproj(bP1, wview(slot2), ti, c0, n, [wev2])
                    sgs = sgB.next()
                    a_ev = act(sgs.ap[:, :n], bP0.ap[:, :n], AF.Copy, waits=[ge, sgs.free])
                    bP0.free = a_ev
                    d_ev = tt(gcv[c][:, PAD + c0:PAD + c0 + n], sgs.ap[:, :n], bP1.ap[:, :n], ALU.mult, [a_ev, ve, pad_ev])
                    bP1.free = d_ev
                    sgs.free = d_ev
                    yield 2.9
                ringM.release(k1, ge)
                ringM.release(k2, ve)
                ts(gcv[c][:, PAD:PAD + HALO], gcv[c][:, PAD:PAD + HALO], mL, 0.0, ALU.mult, ALU.add, [d_ev])
                ev = ts(gcv[c][:, PAD + HALO + OWN:PAD + T], gcv[c][:, PAD + HALO + OWN:PAD + T], mR, 0.0, ALU.mult, ALU.add, [d_ev])
                ev = ts(cv, gcv[c][:, PAD - 1:PAD - 1 + T], vcol(li, V_SCONV + c), 0.0, ALU.mult, ALU.add, [ev, cv_free])
                ev = stt(cv, gcv[c][:, PAD:PAD + T], vcol(li, V_SCONV + 3 + c), cv, ALU.mult, ALU.add, [ev])
                ev = stt(cv, gcv[c][:, PAD + 1:PAD + 1 + T], vcol(li, V_SCONV + 6 + c), cv, ALU.mult, ALU.add, [ev])
                if c == 0:
                    pool_pe, ya_ev = pool_matmuls()
                    p.emit("dve", None, [pool_pe], signal=False)
                    yield 2.0
                k, slot, wev = ringM.next()
                be = None
                for ti, (c0, n) in enumerate(TILES_B):
                    d_b = bS.next()
                    be = inproj(d_b, wview(slot), ti, c0, n, [wev])
                    d2 = tt(Y[2 + c][:, c0:c0 + n], d_b.ap[:, :n], cv[:, c0:c0 + n], ALU.mult, [be, ev])
                    d_b.free = d2
                    cv_free = d2
                    yb_last = d2
                    yield 3.0
                ringM.release(k, be)

            ev = None
            for dc in range(NKC):
                k, slot, wev = ringM.next()
                wo = wview(slot)
                for ti, (c0, n) in enumerate(TILES_B):
                    d_b = bS.next()
                    for kc in range(NKC):
                        w = [wev, d_b.free, yb_last, yc_ev[2], ya_ev] if kc == 0 else []
                        ev = mm(d_b.ap[:, :n], wo[:, kc, :], Y[kc][:, c0:c0 + n], kc == 0, kc == NKC - 1, w, signal=(kc == NKC - 1))
                    d_ev = stt(XS[:, dc, c0:c0 + n], XS[:, dc, c0:c0 + n], ALPHA, d_b.ap[:, :n], ALU.mult, ALU.add, [ev, st["xs_all%d" % S]])
                    d_b.free = d_ev
                    if dc == NKC - 1:
                        y_ev[ti] = d_ev
                    yield 1.9
                ringM.release(k, ev)
            st["HM_free"] = [ev, yb_last]

        def counted(key, gen):
            tot = 0.0
            for c in gen:
                tot += c
                yield c
            counts[key] = tot

        def chain(facts):
            for key, f in facts:
                yield from counted(key, f())

        def run(fkey, fg, bg_facts):
            fg = counted(fkey, fg)
            bg = chain(bg_facts) if bg_facts else None
            nfg = counts_in.get(fkey, 0.0) if counts_in else 0.0
            nbg = sum(counts_in.get(k, 0.0) for k, _ in bg_facts) if counts_in else 0.0
            ratio = (nbg / nfg * BG_BOOST) if nfg else 1.0
            fg_acc = 0.0
            bg_acc = 0.0
            alive = bg is not None
            for c in fg:
                fg_acc += c
                while alive and bg_acc < fg_acc * ratio:
                    try:
                        bg_acc += next(bg)
                    except StopIteration:
                        alive = False
            if alive:
                for _ in bg:
                    pass

        yev = {}

        def Y_(S, li, tag):
            return yev.setdefault((S, li, tag), [None] * 3)

        def hA(li):
            return halo_even(min(HALO, 15 * (nl - li)))

        def hB(li):
            return halo_even(min(HALO, 15 * (nl - li - 1)))

        def f_ffn(S, li, f):
            tag = "f1" if f == 0 else "f2"
            return ffn_phase(S, li, f, Y_(S, li, tag), tiles_for(hA(li) if f == 0 else hB(li)))

        def b_ln(S, li, which):
            tag = ("f1", "mix", "f2")[which]
            tl = tiles_for(hA(li) if which == 0 else hB(li))
            return (("ln", S, li, which), lambda: main_ln(S, li, which, Y_(S, li, tag), tl))

        def b_mix(S, li):
            return (("mix", S, li), lambda: mixer_phase(S, li, Y_(S, li, "mix"), tiles_for(hA(li)), tiles_for(hB(li))))

        for li in range(nl):
            run(("f", 0, li, 0), f_ffn(0, li, 0), [b_ln(1, li - 1, 2)] if li > 0 else [])
            run(("f", 1, li, 0), f_ffn(1, li, 0), [b_ln(0, li, 0), b_mix(0, li), b_ln(0, li, 1)])
            run(("f", 0, li, 1), f_ffn(0, li, 1), [b_ln(1, li, 0), b_mix(1, li), b_ln(1, li, 1)])
            run(("f", 1, li, 1), f_ffn(1, li, 1), [b_ln(0, li, 2)])
        for _ in chain([b_ln(1, nl - 1, 2)]):
            pass

        st_ev = []
        for s in range(NSTREAM):
            st_ev.append(p.emit("sp", lambda e, s=s: e.dma_start(
                out=outd[s].rearrange("p (a t) -> p a t", a=NKC), in_=xs[s][:, :, HALO:HALO + OWN]),
                [xs_ev[s][0], xs_ev[s][1], xs_ev[s][2]], sem=p.sem(f"st{s}"), inc=16))
        p.emit("sp", None, st_ev, signal=False)
        return p, counts

    _, counts0 = emit_all(None)
    p, _ = emit_all(counts0)

    from contextlib import ExitStack
    with ExitStack() as stack:
        for name in list(p.sems.keys()):
            p.sems[name] = stack.enter_context(nc.semaphore(name))
        block = stack.enter_context(nc.Block())

        @block.tensor
        def _(e):
            p.replay("pe", e)

        @block.scalar
        def _(e):
            p.replay("act", e)

        @block.vector
        def _(e):
            p.replay("dve", e)

        @block.gpsimd
        def _(e):
            p.replay("pool", e)

        @block.sync
        def _(e):
            p.replay("sp", e)
    return nc


def _prep_weights(inp, layers):
    nl = len(layers)
    f32 = np.float32

    def blk(w):
        K, M = w.shape
        return w.reshape(K // P, P, M).transpose(1, 0, 2)

    wgu = np.empty((nl * 2 * NJ, P, 2, NKC, P), f32)
    wd = np.empty((nl * 2 * NKC, P, NJ, P), f32)
    win = np.empty((nl * 17, P, NKC, P), f32)
    wout = np.empty((nl * NKC, P, NKC, P), f32)
    wpool = np.zeros((nl, P, 2, P), f32)
    vec = np.zeros((P, nl * NV), f32)
    for i, l in enumerate(layers):
        for f, (kg, ku, kd) in enumerate((("ffn1_w_gate", "ffn1_w_up", "ffn1_w_down"),
                                          ("ffn2_w_gate", "ffn2_w_up", "ffn2_w_down"))):
            g = blk(np.asarray(inp[kg][l]))
            u = blk(np.asarray(inp[ku][l]))
            d = blk(np.asarray(inp[kd][l]))
            base = (i * 2 + f)
            for j in range(NJ):
                wgu[base * NJ + j, :, 0] = g[:, :, j * P:(j + 1) * P]
                wgu[base * NJ + j, :, 1] = u[:, :, j * P:(j + 1) * P]
            for dc in range(NKC):
                wd[base * NKC + dc] = d[:, :, dc * P:(dc + 1) * P]
        wi = blk(np.asarray(inp["mix_w_in"][l]))
        for n_, c in enumerate(WIN_PERM):
            win[i * 17 + n_] = wi[:, :, c * P:(c + 1) * P]
        wo = blk(np.asarray(inp["mix_w_out"][l]))
        for dc in range(NKC):
            wout[i * NKC + dc] = wo[:, :, dc * P:(dc + 1) * P]
        pw = np.asarray(inp["pool_w"][l])
        for c in range(2):
            for hf in range(2):
                wpool[i, hf * 64:(hf + 1) * 64, c, hf * 64:(hf + 1) * 64] = pw[2 * c + hf]
        o = i * NV

        def col(v):
            v = np.asarray(v)
            return v.reshape(-1, P).T

        for w_, (kg, kb) in enumerate((("ln1_g", "ln1_b"), ("ln2_g", "ln2_b"), ("ln3_g", "ln3_b"))):
            vec[:, o + V_LN[w_][0]:o + V_LN[w_][0] + 8] = col(inp[kg][l])
            vec[:, o + V_LN[w_][1]:o + V_LN[w_][1] + 8] = col(inp[kb][l])
        vec[:, o + V_PSCALE:o + V_PSCALE + 2] = col(inp["pool_scale"][l])
        sc = np.asarray(inp["sconv_w"][l])
        for k in range(3):
            vec[:, o + V_SCONV + k * 3:o + V_SCONV + k * 3 + 3] = col(sc[k])
        cc = np.asarray(inp["cconv_w"][l])
        for k in range(31):
            vec[:, o + V_CCONV + k * 3:o + V_CCONV + k * 3 + 3] = col(cc[k])
        vec[:, o + V_CB:o + V_CB + 3] = col(inp["cconv_b"][l])
        vec[:, o + V_CNG:o + V_CNG + 3] = col(inp["cnorm_g"][l])
        vec[:, o + V_CNB:o + V_CNB + 3] = col(inp["cnorm_b"][l])
    return dict(
        wgu=wgu.reshape(nl * 2 * NJ, P, 2048), wd=wd.reshape(nl * 2 * NKC, P, DFF),
        win=win.reshape(nl * 17, P, 1024), wout=wout.reshape(nl * NKC, P, 1024),
        wpool=wpool.reshape(nl, P, 256), vec=vec)


def _aux_tables():
    aux = np.ones((NCORES, NSTREAM, P, 130), np.float32)
    for core in range(NCORES):
        for s in range(NSTREAM):
            q = core * NSTREAM + s
            aux[core, s, :, 0] = 0.0 if q == 0 else 1.0
            aux[core, s, :, 1] = 0.0 if q == NCORES * NSTREAM - 1 else 1.0
            for c in range(2):
                for hf in range(2):
                    w = 2 ** (2 * c + hf + 1)
                    for e_i, e0 in enumerate((EL, ER)):
                        for i in range(32):
                            t = q * OWN - HALO + e0 + i
                            lo = min(max(t - w // 2, 0), SEQ)
                            hi = min(max(t - w // 2 + w, 0), SEQ)
                            cnt = max(hi - lo, 1)
                            aux[core, s, hf * 64:(hf + 1) * 64, 2 + c * 64 + e_i * 32 + i] = 1.0 / cnt
    return aux


def _shard_x(x2d):
    xp = np.zeros((SEQ + 2 * HALO, D), np.float32)
    xp[HALO:HALO + SEQ] = x2d
    outs = []
    for core in range(NCORES):
        arr = np.empty((NSTREAM, P, NKC, T), np.float32)
        for s in range(NSTREAM):
            q = core * NSTREAM + s
            seg = xp[q * OWN:q * OWN + T]
            arr[s] = seg.T.reshape(NKC, P, T).transpose(1, 0, 2)
        outs.append(arr.reshape(NSTREAM, P, NKC * T))
    return outs


def _unshard(res):
    out = np.empty((SEQ, D), np.float32)
    for core in range(NCORES):
        o = np.asarray(res[core]["out"]).reshape(NSTREAM, P, NKC, OWN)
        for s in range(NSTREAM):
            q = core * NSTREAM + s
            out[q * OWN:(q + 1) * OWN] = o[s].transpose(2, 1, 0).reshape(OWN, D)
    return out


_NC_CACHE = {}


def _launch(x2d, inp, layers, final_unscaled=True):
    key = (len(layers), final_unscaled)
    if key not in _NC_CACHE:
        _NC_CACHE[key] = build(len(layers), final_unscaled)
    nc = _NC_CACHE[key]
    w = _prep_weights(inp, layers)
    aux = _aux_tables()
    xs_ = _shard_x(x2d)
    in_maps = []
    for core in range(NCORES):
        m = dict(w)
        m["xin"] = xs_[core]
        m["aux"] = aux[core]
        m["ident"] = np.eye(P, dtype=np.float32)
        in_maps.append(m)
    res = run_bass_kernel_spmd(nc, in_maps, core_ids=list(range(NCORES)))
    return _unshard(res.results)


def kernel(**inputs):
    x = np.asarray(inputs["x"], dtype=np.float32)
    x2d = x.reshape(SEQ, D)
    if FUSED:
        out = _launch(x2d, inputs, list(range(DEPTH)))
    else:
        out = x2d
        for l in range(DEPTH):
            out = _launch(out, inputs, [l])
    return out.reshape(1, SEQ, D).astype(np.float32)
```

```python
import numpy as np
import concourse.bass as bass
import concourse.mybir as mybir
from concourse.bass_utils import run_bass_kernel_spmd

F32 = mybir.dt.float32
BF16 = mybir.dt.bfloat16
AF = mybir.ActivationFunctionType
ALU = mybir.AluOpType

P = 128
DEPTH = 4
D = 1024
DFF = 2816
NJ = DFF // P
NKC = D // P
SEQ = 16384
NCORES = 8
NSTREAM = 2
OWN = SEQ // (NCORES * NSTREAM)
HALO = 60
T = OWN + 2 * HALO
TILES = [(0, 384), (384, 384), (768, 376)]
PAD = 16
TP = T + 2 * PAD
ALPHA = (2.0 * DEPTH) ** 0.25
EPS = 1e-5
NV = 161
NSF = 5
NSM = 3
SLOT = 1024
GROUPS = [(0, 8), (8, 7), (15, 7)]
GMAX = 8
BG_BOOST = 1.0
WIN_PERM = [14, 11, 15, 12, 16, 13, 0, 1, 5, 8, 2, 6, 9, 3, 7, 10, 4]
V_LN = [(0, 8), (16, 24), (32, 40)]
V_PSCALE = 48
V_SCONV = 50
V_CCONV = 59
V_CB = 152
V_CNG = 155
V_CNB = 158
EL = HALO - 16
ER = HALO + OWN - 16

INTERLEAVE = True


def tiles_for(h):
    lo = HALO - h
    W = OWN + 2 * h
    units = W // 2
    base, rem = divmod(units, 3)
    out = []
    c = lo
    for i in range(3):
        n = 2 * (base + (1 if i < rem else 0))
        out.append((c, n))
        c += n
    return out


def halo_even(h):
    return h + (h % 2)
FUSED = True


class Ev:
    __slots__ = ("sem", "val")

    def __init__(self, sem, val):
        self.sem = sem
        self.val = val


class Slot:
    __slots__ = ("ap", "free")

    def __init__(self, ap):
        self.ap = ap
        self.free = None


class Rot:
    def __init__(self, slots):
        self.slots = slots
        self.i = 0

    def next(self):
        s = self.slots[self.i % len(self.slots)]
        self.i += 1
        return s


class Prog:
    ENGS = ("pe", "act", "dve", "pool", "sp")

    def __init__(self, nc):
        self.nc = nc
        self.ops = {e: [] for e in self.ENGS}
        self.sems = {}
        self.cnt = {}
        self.nsem = 0

    def sem(self, name):
        if name not in self.sems:
            self.sems[name] = None
            self.cnt[name] = 0
        return name

    def emit(self, eng, fn, waits=(), sem=None, inc=1, signal=True):
        ws = [(w.sem, w.val) for w in waits if w is not None]
        ev = None
        s = None
        if signal:
            s = sem if sem is not None else self.sem("e_" + eng)
            self.sem(s)
            self.cnt[s] += inc
            ev = Ev(s, self.cnt[s])
        self.ops[eng].append((fn, ws, s, inc))
        return ev

    def replay(self, eng, e):
        waited = {}
        for fn, ws, s, inc in self.ops[eng]:
            for (ws_, wv) in ws:
                if waited.get(ws_, 0) < wv:
                    e.wait_ge(self.sems[ws_], wv)
                    waited[ws_] = wv
            if fn is None:
                continue
            ins = fn(e)
            if s is not None:
                ins.then_inc(self.sems[s], inc)


def build(nl, final_unscaled=True):
    nc = bass.Bass("TRN2", target_bir_lowering=False)

    xin = nc.dram_tensor("xin", [NSTREAM, P, NKC * T], F32, kind="ExternalInput").ap()
    auxd = nc.dram_tensor("aux", [NSTREAM, P, 130], F32, kind="ExternalInput").ap()
    vecd = nc.dram_tensor("vec", [P, nl * NV], F32, kind="ExternalInput").ap()
    wgud = nc.dram_tensor("wgu", [nl * 2 * NJ, P, 2048], F32, kind="ExternalInput").ap()
    wdd = nc.dram_tensor("wd", [nl * 2 * NKC, P, DFF], F32, kind="ExternalInput").ap()
    wind = nc.dram_tensor("win", [nl * 17, P, 1024], F32, kind="ExternalInput").ap()
    woutd = nc.dram_tensor("wout", [nl * NKC, P, 1024], F32, kind="ExternalInput").ap()
    wpoold = nc.dram_tensor("wpool", [nl, P, 256], F32, kind="ExternalInput").ap()
    identd = nc.dram_tensor("ident", [P, P], F32, kind="ExternalInput").ap()
    outd = nc.dram_tensor("out", [NSTREAM, P, NKC * OWN], F32, kind="ExternalOutput").ap()

    A = nc.alloc_sbuf_tensor
    xs = [A(f"xs{s}", [P, NKC, T], F32) for s in range(NSTREAM)]
    xb = [A(f"xb{s}", [P, NKC, T], BF16) for s in range(NSTREAM)]
    HSL = 19
    HM = A("HM", [P, 8 * T + 11712], BF16)
    HF = A("HF", [P, GMAX * T], BF16)
    ringF_t = A("ringF", [P, NSF, SLOT], BF16)
    ringM_t = A("ringM", [P, NSM, SLOT], BF16)
    vecs = A("vecs", [P, nl * NV], F32)
    auxs = [A(f"aux{s}", [P, 130], F32) for s in range(NSTREAM)]
    sgF_t = A("sgF", [P, 2, 384], F32)
    sgB_t = A("sgB", [P, 2, 384], F32)
    sq_t = A("sq", [P, 4, 384], BF16)
    nt_t = A("nt", [P, 2, 384], F32)
    mean_sb = A("mean_sb", [P, T], F32)
    var_sb = A("var_sb", [P, T], F32)
    ones_t = A("ones", [P, P], BF16)
    cst = A("cst", [P, 4], F32)
    ident_f = A("ident_f", [P, P], F32)
    dg_t = A("dg", [P, 2, P], BF16)
    banks = [nc.alloc_psum_tensor(f"bank{i}", [P, 512], F32) for i in range(8)]

    def Yslot(i):
        return HM[:, i * T:(i + 1) * T]

    def Hslot(i):
        return HF[:, i * T:(i + 1) * T]

    TA = HM[:, 8 * T:8 * T + 11712].bitcast(F32)

    def vcol(li, off):
        c = li * NV + off
        return vecs[:, c:c + 1]

    def emit_all(counts_in):
        p = Prog(nc)
        counts = {}
        sgF = Rot([Slot(sgF_t[:, i, :]) for i in range(2)])
        sgB = Rot([Slot(sgB_t[:, i, :]) for i in range(2)])
        sqring = Rot([Slot(sq_t[:, i, :]) for i in range(4)])
        ntring = Rot([Slot(nt_t[:, i, :]) for i in range(2)])
        dgring = Rot([Slot(dg_t[:, i, :]) for i in range(2)])
        bs = [Slot(b) for b in banks]
        bG = Rot([bs[0], bs[1]])
        bU = Rot([bs[2], bs[3]])
        bDn = Rot([bs[0], bs[1], bs[2], bs[3]])
        bP0 = bs[4]
        bP1 = bs[5]
        bS = Rot([bs[4], bs[5]])
        bM = bs[6]
        bQ = bs[7]

        class Ring:
            def __init__(self, tens, ns, name):
                self.t, self.ns, self.name = tens, ns, name
                self.plan = []
                self.issued = 0
                self.consumed = 0
                self.rel = {}
                self.ev = {}

            def issue_upto(self, k):
                while self.issued <= k and self.issued < len(self.plan):
                    m = self.issued
                    src, n = self.plan[m]
                    sl = m % self.ns
                    waits = []
                    if m >= self.ns:
                        if (m - self.ns) not in self.rel:
                            break
                        waits.append(self.rel[m - self.ns])
                    dst = self.t[:, sl, 0:n]
                    self.ev[m] = p.emit(
                        "pool", (lambda e, dst=dst, src=src: e.dma_start(out=dst, in_=src)),
                        waits, sem=p.sem(f"{self.name}{sl}"), inc=16)
                    self.issued += 1

            def next(self):
                k = self.consumed
                self.issue_upto(k + self.ns - 1)
                assert self.issued > k, "ring: load not issuable (missing release)"
                self.consumed += 1
                return k, self.t[:, k % self.ns, :], self.ev[k]

            def release(self, k, ev):
                self.rel[k] = ev

        ringF = Ring(ringF_t, NSF, "rf")
        ringM = Ring(ringM_t, NSM, "rm")

        def ffn_loads(li, f):
            base = (li * 2 + f)
            out = []
            for (j0, nj) in GROUPS:
                for jj in range(nj):
                    row = wgud[base * NJ + j0 + jj]
                    out.append((row[:, 0:1024], 1024))
                    out.append((row[:, 1024:2048], 1024))
                for dc in range(NKC):
                    out.append((wdd[base * NKC + dc][:, j0 * P:(j0 + nj) * P], nj * P))
            return out

        def mix_loads(li):
            b = li * 17
            out = [(wind[b + i], 1024) for i in range(8)]
            out.append((wpoold[li], 256))
            out += [(wind[b + i], 1024) for i in range(8, 17)]
            for dc in range(NKC):
                out.append((woutd[li * NKC + dc], 1024))
            return out

        for li in range(nl):
            for S in range(NSTREAM):
                ringF.plan += ffn_loads(li, 0)
            for S in range(NSTREAM):
                ringF.plan += ffn_loads(li, 1)
            for S in range(NSTREAM):
                ringM.plan += mix_loads(li)

        def mm(out, lhsT, rhs, start, stop, waits=(), signal=False):
            return p.emit("pe", lambda e: e.matmul(out, lhsT, rhs, start=start, stop=stop), waits, signal=signal)

        def act(out, in_, func, scale=1.0, bias=0.0, waits=()):
            return p.emit("act", lambda e: e.activation(out=out, in_=in_, func=func, bias=bias, scale=scale), waits)

        def tt(out, in0, in1, op, waits=()):
            return p.emit("dve", lambda e: e.tensor_tensor(out=out, in0=in0, in1=in1, op=op), waits)

        def stt(out, in0, scalar, in1, op0, op1, waits=()):
            return p.emit("dve", lambda e: e.scalar_tensor_tensor(out=out, in0=in0, scalar=scalar, in1=in1, op0=op0, op1=op1), waits)

        def ts(out, in0, s1, s2, op0, op1, waits=()):
            return p.emit("dve", lambda e: e.tensor_scalar(out=out, in0=in0, scalar1=s1, scalar2=s2, op0=op0, op1=op1), waits)

        def memset(out, val, waits=()):
            return p.emit("dve", lambda e: e.memset(out, val), waits)

        xb_ev = [[None] * 3 for _ in range(NSTREAM)]
        xs_ev = [[None] * 3 for _ in range(NSTREAM)]
        st = {"HF_free": [], "HM_free": [], "ln_free": None, "xs_all0": None, "xs_all1": None}

        ld_ev = []
        for s in range(NSTREAM):
            ld_ev.append(p.emit("sp", lambda e, s=s: e.dma_start(out=xs[s][:].rearrange("p a t -> p (a t)"), in_=xin[s]),
                                sem=p.sem(f"ldx{s}"), inc=16))
        aux_ev = [p.emit("sp", lambda e, s=s: e.dma_start(out=auxs[s][:], in_=auxd[s]), sem=p.sem(f"lda{s}"), inc=16)
                  for s in range(NSTREAM)]
        vec_ev = p.emit("sp", lambda e: e.dma_start(out=vecs[:], in_=vecd), sem=p.sem("ldv"), inc=16)
        ident_ev = p.emit("sp", lambda e: e.dma_start(out=ident_f[:], in_=identd), sem=p.sem("ldi"), inc=16)
        memset(ones_t[:], 1.0)
        memset(cst[0:64, 0:1], 0.5)
        memset(cst[64:128, 0:1], 0.25)
        memset(cst[0:64, 1:2], 0.125)
        const_ev = memset(cst[64:128, 1:2], 0.0625)
        for s in range(NSTREAM):
            e1 = None
            for dc in range(NKC):
                e1 = act(xb[s][:, dc, :], xs[s][:, dc, :], AF.Copy, waits=[ld_ev[s]])
            for ti in range(3):
                xb_ev[s][ti] = e1
                xs_ev[s][ti] = ld_ev[s]

        def ln_task(nch, inv_n, src, in_ev, out_fn, tiles):
            first = True
            d1 = a1 = None
            for ti, (c0, n) in enumerate(tiles):
                pe_ev = None

                def stat_mm(c, r1, r2, e1, e2, n=n):
                    m1 = mm(bM.ap[:, :n], ones_t[:], r1.ap[:, :n], c == 0, c == nch - 1,
                            [e1] + ([bM.free] if c == 0 else []), signal=True)
                    m2 = mm(bQ.ap[:, :n], ones_t[:], r2.ap[:, :n], c == 0, c == nch - 1,
                            [e2] + ([bQ.free] if c == 0 else []), signal=True)
                    r1.free = m1
                    r2.free = m2
                    return m2

                pend = None
                for c in range(nch):
                    r1 = sqring.next()
                    w = [in_ev[ti], r1.free]
                    if first:
                        w += [const_ev, vec_ev]
                    e1 = p.emit("dve", lambda e, o_=r1.ap[:, :n], i_=src(c, c0, n): e.tensor_copy(out=o_, in_=i_), w)
                    r2 = sqring.next()
                    e2 = act(r2.ap[:, :n], src(c, c0, n), AF.Square, waits=[in_ev[ti], r2.free])
                    if pend is not None:
                        pe_ev = stat_mm(*pend)
                    pend = (c, r1, r2, e1, e2)
                    first = False
                    yield 0.7
                pe_ev = stat_mm(*pend)
                w = [pe_ev]
                if ti == 0 and st["ln_free"] is not None:
                    w.append(st["ln_free"])
                a1 = act(mean_sb[:, c0:c0 + n], bM.ap[:, :n], AF.Identity, scale=inv_n, waits=w)
                nt = ntring.next()
                a2 = act(nt.ap[:, :n], bM.ap[:, :n], AF.Square, scale=inv_n, waits=[nt.free])
                bM.free = a2
                d1 = stt(var_sb[:, c0:c0 + n], bQ.ap[:, :n], inv_n, nt.ap[:, :n], ALU.mult, ALU.subtract,
                         [a2, pe_ev] + ([st["ln_free"]] if ti == 0 else []))
                bQ.free = d1
                nt.free = d1
                yield 1.2
            d2 = ts(var_sb[:], var_sb[:], 0.0, EPS, ALU.max, ALU.add, [d1])
            a3 = act(var_sb[:], var_sb[:], AF.Sqrt, waits=[d2])
            d3 = p.emit("dve", lambda e: e.reciprocal(out=var_sb[:], in_=var_sb[:]), [a3])
            d4 = stt(mean_sb[:], mean_sb[:], -1.0, var_sb[:], ALU.mult, ALU.mult, [d3, a1])
            yield 5.0
            last = None
            for ti, (c0, n) in enumerate(tiles):
                for c in range(nch):
                    nt = ntring.next()
                    e1 = tt(nt.ap[:, :n], src(c, c0, n), var_sb[:, c0:c0 + n], ALU.mult, [d4, nt.free])
                    e2 = tt(nt.ap[:, :n], nt.ap[:, :n], mean_sb[:, c0:c0 + n], ALU.add, [e1])
                    last = e2
                    nt.free = out_fn(c, ti, c0, n, nt.ap[:, :n], e2)
                    yield 1.3
            st["ln_free"] = last

        def main_ln(S, li, which, y_ev, tiles):
            goff, boff = V_LN[which]

            def src(c, c0, n):
                return xs[S][:, c, c0:c0 + n]

            def out_fn(c, ti, c0, n, nt_ap, ev):
                e1 = act(xs[S][:, c, c0:c0 + n], nt_ap, AF.Identity, scale=vcol(li, goff + c), bias=vcol(li, boff + c), waits=[ev])
                e2 = act(xb[S][:, c, c0:c0 + n], nt_ap, AF.Identity, scale=vcol(li, goff + c), bias=vcol(li, boff + c), waits=[ev])
                xs_ev[S][ti] = e1
                xb_ev[S][ti] = e2
                st["xs_all%d" % S] = e1
                return e2

            return ln_task(NKC, 1.0 / D, src, y_ev, out_fn, tiles)

        def h_sync(key):
            if st[key]:
                p.emit("act", None, st[key], signal=False)
                p.emit("dve", None, st[key], signal=False)
            st[key] = []

        def ffn_phase(S, li, f, y_ev, TILES):
            X = xb[S]
            XS = xs[S]
            h_sync("HF_free")
            hfree = None
            last_down = None
            h_last_dve = None
            for gi, (j0, nj) in enumerate(GROUPS):
                h_ev = [[None] * 3 for _ in range(nj)]
                for jj in range(nj):
                    kg, slot_g, wev_g = ringF.next()
                    ku, slot_u, wev_u = ringF.next()
                    wg = slot_g.rearrange("p (k m) -> p k m", k=NKC)
                    wu = slot_u.rearrange("p (k m) -> p k m", k=NKC)
                    up_ev = gate_ev = None
                    for ti, (c0, n) in enumerate(TILES):
                        g_b = bG.next()
                        u_b = bU.next()
                        for kc in range(NKC):
                            w = [wev_g, xb_ev[S][ti], g_b.free] if kc == 0 else []
                            gate_ev = mm(g_b.ap[:, :n], wg[:, kc, :], X[:, kc, c0:c0 + n], kc == 0, kc == NKC - 1, w, signal=(kc == NKC - 1))
                        for kc in range(NKC):
                            w = [wev_u, u_b.free] if kc == 0 else []
                            up_ev = mm(u_b.ap[:, :n], wu[:, kc, :], X[:, kc, c0:c0 + n], kc == 0, kc == NKC - 1, w, signal=(kc == NKC - 1))
                        sgs = sgF.next()
                        a_ev = act(sgs.ap[:, :n], g_b.ap[:, :n], AF.Silu, waits=[gate_ev, sgs.free])
                        g_b.free = a_ev
                        d_ev = stt(Hslot(jj)[:, c0:c0 + n], sgs.ap[:, :n], 0.5, u_b.ap[:, :n], ALU.mult, ALU.mult, [a_ev, up_ev, hfree])
                        u_b.free = d_ev
                        sgs.free = d_ev
                        h_ev[jj][ti] = d_ev
                        h_last_dve = d_ev
                        yield 2.9
                    ringF.release(kg, gate_ev)
                    ringF.release(ku, up_ev)
                for dc in range(NKC):
                    k, slot, wev = ringF.next()
                    ev = None
                    for ti, (c0, n) in enumerate(TILES):
                        d_b = bDn.next()
                        for kk in range(nj):
                            w = [wev, d_b.free, h_ev[nj - 1][ti]] if kk == 0 else []
                            ev = mm(d_b.ap[:, :n], slot[:, kk * P:(kk + 1) * P], Hslot(kk)[:, c0:c0 + n], kk == 0, kk == nj - 1, w, signal=(kk == nj - 1))
                        if gi == 0:
                            d_ev = stt(XS[:, dc, c0:c0 + n], XS[:, dc, c0:c0 + n], ALPHA, d_b.ap[:, :n], ALU.mult, ALU.add, [ev, xs_ev[S][ti]])
                        else:
                            d_ev = tt(XS[:, dc, c0:c0 + n], XS[:, dc, c0:c0 + n], d_b.ap[:, :n], ALU.add, [ev, y_ev[ti]])
                        d_b.free = d_ev
                        if dc == NKC - 1:
                            y_ev[ti] = d_ev
                        yield 0.18 * nj + 0.1
                    ringF.release(k, ev)
                    last_down = ev
                hfree = last_down
            st["HF_free"] = [last_down, h_last_dve]

        def mixer_phase(S, li, y_ev, TILES_A, TILES_B):
            X = xb[S]
            XS = xs[S]
            AX = auxs[S]
            mL = AX[:, 0:1]
            mR = AX[:, 1:2]
            h_sync("HM_free")
            Y = [Yslot(i) for i in range(8)]

            def wview(slot):
                return slot.rearrange("p (k m) -> p k m", k=NKC)

            def inproj(bank, wv_c, ti, c0, n, extra):
                ev = None
                for kc in range(NKC):
                    w = ([xb_ev[S][ti], bank.free] + extra) if kc == 0 else []
                    ev = mm(bank.ap[:, :n], wv_c[:, kc, :], X[:, kc, c0:c0 + n], kc == 0, kc == NKC - 1, w, signal=(kc == NKC - 1))
                return ev

            ABF = HM[:, 8 * T:8 * T + 3 * TP]
            abuf = [ABF[:, c * TP:(c + 1) * TP] for c in range(3)]
            CO0 = (3 * TP) // 2
            cobuf = [TA[:, CO0 + c * T:CO0 + (c + 1) * T] for c in range(3)]
            a3 = ABF.rearrange("p (c t) -> p c t", c=3)
            memset(a3[:, :, 0:PAD], 0.0)
            pad_ev = memset(a3[:, :, TP - PAD:TP], 0.0)
            a_done = [None] * 3
            for c in range(3):
                k1, slot1, wev1 = ringM.next()
                k2, slot2, wev2 = ringM.next()
                ge = ve = d_ev = None
                for ti, (c0, n) in enumerate(TILES_A):
                    ge = inproj(bP0, wview(slot1), ti, c0, n, [wev1])
                    ve = inproj(bP1, wview(slot2), ti, c0, n, [wev2])
                    sgs = sgB.next()
                    a_ev = act(sgs.ap[:, :n], bP0.ap[:, :n], AF.Tanh, scale=0.5, waits=[ge, sgs.free])
                    bP0.free = a_ev
                    d_ev = stt(abuf[c][:, PAD + c0:PAD + c0 + n], sgs.ap[:, :n], 1.0, bP1.ap[:, :n], ALU.add, ALU.mult, [a_ev, ve, pad_ev])
                    bP1.free = d_ev
                    sgs.free = d_ev
                    yield 2.9
                ringM.release(k1, ge)
                ringM.release(k2, ve)
                ts(abuf[c][:, PAD:PAD + HALO], abuf[c][:, PAD:PAD + HALO], mL, 0.0, ALU.mult, ALU.add, [d_ev, aux_ev[S]])
                a_done[c] = ts(abuf[c][:, PAD + HALO + OWN:PAD + T], abuf[c][:, PAD + HALO + OWN:PAD + T], mR, 0.0, ALU.mult, ALU.add, [d_ev])
            cbanks = [bs[4], bs[5], bs[6]]
            co_ev = [None] * 3
            for c in range(3):
                last_me = [None] * 3

                def build_diag(kk, c=c):
                    dg_ = dgring.next()
                    out_ap, w_ap = dg_.ap, vcol(li, V_CCONV + kk * 3 + c)
                    de_ = p.emit("pool", lambda e: e.tensor_scalar(out=out_ap, in0=ident_f[:], scalar1=w_ap, scalar2=0.0,
                                                                    op0=ALU.mult, op1=ALU.add),
                                 [dg_.free, vec_ev, ident_ev])
                    return dg_, de_

                nxt = build_diag(0)
                for kk in range(31):
                    dg, de = nxt
                    if kk + 1 < 31:
                        nxt = build_diag(kk + 1)
                    me = None
                    for ti, (c0, n) in enumerate(TILES_B):
                        w = [de] + ([a_done[c], cbanks[ti].free] if kk == 0 else [])
                        me = mm(cbanks[ti].ap[:, :n], dg.ap, abuf[c][:, PAD + c0 + kk - 15:PAD + c0 + kk - 15 + n], kk == 0, kk == 30, w,
                                signal=(kk == 30 or ti == 2))
                        if kk == 30:
                            last_me[ti] = me
                    dg.free = me
                    yield 0.6
                for ti, (c0, n) in enumerate(TILES_B):
                    ae = act(cobuf[c][:, c0:c0 + n], cbanks[ti].ap[:, :n], AF.Identity, scale=0.5, bias=vcol(li, V_CB + c), waits=[last_me[ti]])
                    cbanks[ti].free = ae
                    co_ev[ti] = ae
                yield 1.5
            yc_ev = [None] * 3

            def c_src(c, c0, n):
                return cobuf[c][:, c0:c0 + n]

            def c_out(c, ti, c0, n, nt_ap, ev):
                e = act(Y[5 + c][:, c0:c0 + n], nt_ap, AF.Silu, scale=vcol(li, V_CNG + c), bias=vcol(li, V_CNB + c), waits=[ev])
                yc_ev[ti] = e
                return e

            for cst_ in ln_task(3, 1.0 / 384.0, c_src, co_ev, c_out, TILES_B):
                yield cst_
            c_done_dve = st["ln_free"]
            c_done_act = yc_ev[2]

            ub = [TA[:, c * TP:(c + 1) * TP] for c in range(2)]
            s_a = TA[:, 2 * TP:3 * TP]
            s_b = TA[:, 3 * TP:4 * TP]
            tot = TA[:, 4 * TP:4 * TP + T]
            u2 = TA[:, 0:2 * TP].rearrange("p (c t) -> p c t", c=2)
            p.emit("act", None, [c_done_dve], signal=False)
            p.emit("dve", None, [c_done_act], signal=False)
            memset(u2[:, :, 0:PAD], 0.0)
            pad_ev = memset(u2[:, :, TP - PAD:TP], 0.0)
            u_done = [None] * 2
            for c in range(2):
                k, slot, wev = ringM.next()
                a_ev = ue = None
                for ti, (c0, n) in enumerate(TILES_A):
                    d_b = bS.next()
                    ue = inproj(d_b, wview(slot), ti, c0, n, [wev])
                    a_ev = act(ub[c][:, PAD + c0:PAD + c0 + n], d_b.ap[:, :n], AF.Copy, waits=[ue, pad_ev, c_done_dve])
                    d_b.free = a_ev
                    yield 1.5
                ringM.release(k, ue)
                ts(ub[c][:, PAD:PAD + HALO], ub[c][:, PAD:PAD + HALO], mL, 0.0, ALU.mult, ALU.add, [a_ev, aux_ev[S]])
                u_done[c] = ts(ub[c][:, PAD + HALO + OWN:PAD + T], ub[c][:, PAD + HALO + OWN:PAD + T], mR, 0.0, ALU.mult, ALU.add, [a_ev])
            k, slot, wev = ringM.next()
            pw = slot[:, 0:256].rearrange("p (c m) -> p c m", c=2)
            pooled = [Y[2], Y[3]]
            pool_dve = None
            ya_ev = None
            pool_ready = [None, None]
            for c in range(2):
                ev = u_done[c]
                for hf in range(2):
                    r0, r1 = hf * 64, hf * 64 + 64
                    gidx = 2 * c + hf
                    w_ = 2 ** (gidx + 1)
                    srcb = ub[c]
                    bufs = [s_a, s_b]
                    step = 1
                    nb = 0
                    while step * 2 < w_:
                        dst = bufs[nb % 2]
                        L = TP - step
                        ev = tt(dst[r0:r1, 0:L], srcb[r0:r1, 0:L], srcb[r0:r1, step:step + L], ALU.add, [ev])
                        srcb = dst
                        nb += 1
                        step *= 2
                    h2 = w_ // 2
                    ev = tt(tot[r0:r1, :], srcb[r0:r1, PAD - h2:PAD - h2 + T], srcb[r0:r1, PAD:PAD + T], ALU.add, [ev])
                ev = stt(pooled[c][:, :], tot[:, :], cst[:, c:c + 1], ub[c][:, PAD:PAD + T], ALU.mult, ALU.subtract, [ev, const_ev])
                for e_i, e0 in enumerate((EL, ER)):
                    nt = ntring.next()
                    e1 = tt(nt.ap[:, 0:32], tot[:, e0:e0 + 32], AX[:, 2 + c * 64 + e_i * 32:2 + c * 64 + e_i * 32 + 32], ALU.mult, [ev, nt.free])
                    ev = tt(pooled[c][:, e0:e0 + 32], nt.ap[:, 0:32], ub[c][:, PAD + e0:PAD + e0 + 32], ALU.subtract, [e1])
                    nt.free = ev
                yield 8.0
                pool_ready[c] = ev
                pool_dve = ev
            pool_k, pool_wev = k, wev

            def pool_matmuls():
                me = ya = None
                for c in range(2):
                    for ti, (c0, n) in enumerate(TILES_B):
                        d_b = bS.next()
                        me = mm(d_b.ap[:, :n], pw[:, c, :], pooled[c][:, c0:c0 + n], True, True, [pool_wev, pool_ready[c], d_b.free], signal=True)
                        ya = act(Y[c][:, c0:c0 + n], d_b.ap[:, :n], AF.Identity, scale=vcol(li, V_PSCALE + c), waits=[me])
                        d_b.free = ya
                ringM.release(pool_k, me)
                return me, ya

            gcv = [TA[:, c * TP:(c + 1) * TP] for c in range(3)]
            cv = TA[:, 3 * TP:3 * TP + T]
            g3 = TA[:, 0:3 * TP].rearrange("p (c t) -> p c t", c=3)
            memset(g3[:, :, 0:PAD], 0.0, [pool_dve])
            pad_ev = memset(g3[:, :, TP - PAD:TP], 0.0)
            cv_free = None
            yb_last = None
            for c in range(3):
                k1, slot1, wev1 = ringM.next()
                k2, slot2, wev2 = ringM.next()
                ge = ve = d_ev = None
                for ti, (c0, n) in enumerate(TILES_A):
                    ge = inproj(bP0, wview(slot1), ti, c0, n, [wev1])
                    ve = inproj(bP1, wview(slot2), ti, c0, n, [wev2])
                    sgs = sgB.next()
                    a_ev = act(sgs.ap[:, :n], bP0.ap[:, :n], AF.Copy, waits=[ge, sgs.free])
                    bP0.free = a_ev
                    d_ev = tt(gcv[c][:, PAD + c0:PAD + c0 + n], sgs.ap[:, :n], bP1.ap[:, :n], ALU.mult, [a_ev, ve, pad_ev])
                    bP1.free = d_ev
                    sgs.free = d_ev
                    yield 2.9
                ringM.release(k1, ge)
                ringM.release(k2, ve)
                ts(gcv[c][:, PAD:PAD + HALO], gcv[c][:, PAD:PAD + HALO], mL, 0.0, ALU.mult, ALU.add, [d_ev])
                ev = ts(gcv[c][:, PAD + HALO + OWN:PAD + T], gcv[c][:, PAD + HALO + OWN:PAD + T], mR, 0.0, ALU.mult, ALU.add, [d_ev])
                ev = ts(cv, gcv[c][:, PAD - 1:PAD - 1 + T], vcol(li, V_SCONV + c), 0.0, ALU.mult, ALU.add, [ev, cv_free])
                ev = stt(cv, gcv[c][:, PAD:PAD + T], vcol(li, V_SCONV + 3 + c), cv, ALU.mult, ALU.add, [ev])
                ev = stt(cv, gcv[c][:, PAD + 1:PAD + 1 + T], vcol(li, V_SCONV + 6 + c), cv, ALU.mult, ALU.add, [ev])
                if c == 0:
                    pool_pe, ya_ev = pool_matmuls()
                    p.emit("dve", None, [pool_pe], signal=False)
                    yield 2.0
                k, slot, wev = ringM.next()
                be = None
                for ti, (c0, n) in enumerate(TILES_B):
                    d_b = bS.next()
                    be = inproj(d_b, wview(slot), ti, c0, n, [wev])
                    d2 = tt(Y[2 + c][:, c0:c0 + n], d_b.ap[:, :n], cv[:, c0:c0 + n], ALU.mult, [be, ev])
                    d_b.free = d2
                    cv_free = d2
                    yb_last = d2
                    yield 3.0
                ringM.release(k, be)

            ev = None
            for dc in range(NKC):
                k, slot, wev = ringM.next()
                wo = wview(slot)
                for ti, (c0, n) in enumerate(TILES_B):
                    d_b = bS.next()
                    for kc in range(NKC):
                        w = [wev, d_b.free, yb_last, yc_ev[2], ya_ev] if kc == 0 else []
                        ev = mm(d_b.ap[:, :n], wo[:, kc, :], Y[kc][:, c0:c0 + n], kc == 0, kc == NKC - 1, w, signal=(kc == NKC - 1))
                    d_ev = stt(XS[:, dc, c0:c0 + n], XS[:, dc, c0:c0 + n], ALPHA, d_b.ap[:, :n], ALU.mult, ALU.add, [ev, st["xs_all%d" % S]])
                    d_b.free = d_ev
                    if dc == NKC - 1:
                        y_ev[ti] = d_ev
                    yield 1.9
                ringM.release(k, ev)
            st["HM_free"] = [ev, yb_last]

        def counted(key, gen):
            tot = 0.0
            for c in gen:
                tot += c
                yield c
            counts[key] = tot

        def chain(facts):
            for key, f in facts:
                yield from counted(key, f())

        def run(fkey, fg, bg_facts):
            fg = counted(fkey, fg)
            bg = chain(bg_facts) if bg_facts else None
            nfg = counts_in.get(fkey, 0.0) if counts_in else 0.0
            nbg = sum(counts_in.get(k, 0.0) for k, _ in bg_facts) if counts_in else 0.0
            ratio = (nbg / nfg * BG_BOOST) if nfg else 1.0
            fg_acc = 0.0
            bg_acc = 0.0
            alive = bg is not None
            for c in fg:
                fg_acc += c
                while alive and bg_acc < fg_acc * ratio:
                    try:
                        bg_acc += next(bg)
                    except StopIteration:
                        alive = False
            if alive:
                for _ in bg:
                    pass

        yev = {}

        def Y_(S, li, tag):
            return yev.setdefault((S, li, tag), [None] * 3)

        def hA(li):
            return halo_even(min(HALO, 15 * (nl - li)))

        def hB(li):
            return halo_even(min(HALO, 15 * (nl - li - 1)))

        def f_ffn(S, li, f):
            tag = "f1" if f == 0 else "f2"
            return ffn_phase(S, li, f, Y_(S, li, tag), tiles_for(hA(li) if f == 0 else hB(li)))

        def b_ln(S, li, which):
            tag = ("f1", "mix", "f2")[which]
            tl = tiles_for(hA(li) if which == 0 else hB(li))
            return (("ln", S, li, which), lambda: main_ln(S, li, which, Y_(S, li, tag), tl))

        def b_mix(S, li):
            return (("mix", S, li), lambda: mixer_phase(S, li, Y_(S, li, "mix"), tiles_for(hA(li)), tiles_for(hB(li))))

        for li in range(nl):
            run(("f", 0, li, 0), f_ffn(0, li, 0), [b_ln(1, li - 1, 2)] if li > 0 else [])
            run(("f", 1, li, 0), f_ffn(1, li, 0), [b_ln(0, li, 0), b_mix(0, li), b_ln(0, li, 1)])
            run(("f", 0, li, 1), f_ffn(0, li, 1), [b_ln(1, li, 0), b_mix(1, li), b_ln(1, li, 1)])
            run(("f", 1, li, 1), f_ffn(1, li, 1), [b_ln(0, li, 2)])
        for _ in chain([b_ln(1, nl - 1, 2)]):
            pass

        st_ev = []
        for s in range(NSTREAM):
            st_ev.append(p.emit("sp", lambda e, s=s: e.dma_start(
                out=outd[s].rearrange("p (a t) -> p a t", a=NKC), in_=xs[s][:, :, HALO:HALO + OWN]),
                [xs_ev[s][0], xs_ev[s][1], xs_ev[s][2]], sem=p.sem(f"st{s}"), inc=16))
        p.emit("sp", None, st_ev, signal=False)
        return p, counts

    _, counts0 = emit_all(None)
    p, _ = emit_all(counts0)

    from contextlib import ExitStack
    with ExitStack() as stack:
        for name in list(p.sems.keys()):
            p.sems[name] = stack.enter_context(nc.semaphore(name))
        block = stack.enter_context(nc.Block())

        @block.tensor
        def _(e):
            p.replay("pe", e)

        @block.scalar
        def _(e):
            p.replay("act", e)

        @block.vector
        def _(e):
            p.replay("dve", e)

        @block.gpsimd
        def _(e):
            p.replay("pool", e)

        @block.sync
        def _(e):
            p.replay("sp", e)
    return nc


def _prep_weights(inp, layers):
    nl = len(layers)
    f32 = np.float32

    def blk(w):
        K, M = w.shape
        return w.reshape(K // P, P, M).transpose(1, 0, 2)

    wgu = np.empty((nl * 2 * NJ, P, 2, NKC, P), f32)
    wd = np.empty((nl * 2 * NKC, P, NJ, P), f32)
    win = np.empty((nl * 17, P, NKC, P), f32)
    wout = np.empty((nl * NKC, P, NKC, P), f32)
    wpool = np.zeros((nl, P, 2, P), f32)
    vec = np.zeros((P, nl * NV), f32)
    for i, l in enumerate(layers):
        for f, (kg, ku, kd) in enumerate((("ffn1_w_gate", "ffn1_w_up", "ffn1_w_down"),
                                          ("ffn2_w_gate", "ffn2_w_up", "ffn2_w_down"))):
            g = blk(np.asarray(inp[kg][l]))
            u = blk(np.asarray(inp[ku][l]))
            d = blk(np.asarray(inp[kd][l]))
            base = (i * 2 + f)
            for j in range(NJ):
                wgu[base * NJ + j, :, 0] = g[:, :, j * P:(j + 1) * P]
                wgu[base * NJ + j, :, 1] = u[:, :, j * P:(j + 1) * P]
            for dc in range(NKC):
                wd[base * NKC + dc] = d[:, :, dc * P:(dc + 1) * P]
        wi = blk(np.asarray(inp["mix_w_in"][l]))
        for n_, c in enumerate(WIN_PERM):
            win[i * 17 + n_] = wi[:, :, c * P:(c + 1) * P]
        wo = blk(np.asarray(inp["mix_w_out"][l]))
        for dc in range(NKC):
            wout[i * NKC + dc] = wo[:, :, dc * P:(dc + 1) * P]
        pw = np.asarray(inp["pool_w"][l])
        for c in range(2):
            for hf in range(2):
                wpool[i, hf * 64:(hf + 1) * 64, c, hf * 64:(hf + 1) * 64] = pw[2 * c + hf]
        o = i * NV

        def col(v):
            v = np.asarray(v)
            return v.reshape(-1, P).T

        for w_, (kg, kb) in enumerate((("ln1_g", "ln1_b"), ("ln2_g", "ln2_b"), ("ln3_g", "ln3_b"))):
            vec[:, o + V_LN[w_][0]:o + V_LN[w_][0] + 8] = col(inp[kg][l])
            vec[:, o + V_LN[w_][1]:o + V_LN[w_][1] + 8] = col(inp[kb][l])
        vec[:, o + V_PSCALE:o + V_PSCALE + 2] = col(inp["pool_scale"][l])
        sc = np.asarray(inp["sconv_w"][l])
        for k in range(3):
            vec[:, o + V_SCONV + k * 3:o + V_SCONV + k * 3 + 3] = col(sc[k])
        cc = np.asarray(inp["cconv_w"][l])
        for k in range(31):
            vec[:, o + V_CCONV + k * 3:o + V_CCONV + k * 3 + 3] = col(cc[k])
        vec[:, o + V_CB:o + V_CB + 3] = col(inp["cconv_b"][l])
        vec[:, o + V_CNG:o + V_CNG + 3] = col(inp["cnorm_g"][l])
        vec[:, o + V_CNB:o + V_CNB + 3] = col(inp["cnorm_b"][l])
    return dict(
        wgu=wgu.reshape(nl * 2 * NJ, P, 2048), wd=wd.reshape(nl * 2 * NKC, P, DFF),
        win=win.reshape(nl * 17, P, 1024), wout=wout.reshape(nl * NKC, P, 1024),
        wpool=wpool.reshape(nl, P, 256), vec=vec)


def _aux_tables():
    aux = np.ones((NCORES, NSTREAM, P, 130), np.float32)
    for core in range(NCORES):
        for s in range(NSTREAM):
            q = core * NSTREAM + s
            aux[core, s, :, 0] = 0.0 if q == 0 else 1.0
            aux[core, s, :, 1] = 0.0 if q == NCORES * NSTREAM - 1 else 1.0
            for c in range(2):
                for hf in range(2):
                    w = 2 ** (2 * c + hf + 1)
                    for e_i, e0 in enumerate((EL, ER)):
                        for i in range(32):
                            t = q * OWN - HALO + e0 + i
                            lo = min(max(t - w // 2, 0), SEQ)
                            hi = min(max(t - w // 2 + w, 0), SEQ)
                            cnt = max(hi - lo, 1)
                            aux[core, s, hf * 64:(hf + 1) * 64, 2 + c * 64 + e_i * 32 + i] = 1.0 / cnt
    return aux


def _shard_x(x2d):
    xp = np.zeros((SEQ + 2 * HALO, D), np.float32)
    xp[HALO:HALO + SEQ] = x2d
    outs = []
    for core in range(NCORES):
        arr = np.empty((NSTREAM, P, NKC, T), np.float32)
        for s in range(NSTREAM):
            q = core * NSTREAM + s
            seg = xp[q * OWN:q * OWN + T]
            arr[s] = seg.T.reshape(NKC, P, T).transpose(1, 0, 2)
        outs.append(arr.reshape(NSTREAM, P, NKC * T))
    return outs


def _unshard(res):
    out = np.empty((SEQ, D), np.float32)
    for core in range(NCORES):
        o = np.asarray(res[core]["out"]).reshape(NSTREAM, P, NKC, OWN)
        for s in range(NSTREAM):
            q = core * NSTREAM + s
            out[q * OWN:(q + 1) * OWN] = o[s].transpose(2, 1, 0).reshape(OWN, D)
    return out


_NC_CACHE = {}


def _launch(x2d, inp, layers, final_unscaled=True):
    key = (len(layers), final_unscaled)
    if key not in _NC_CACHE:
        _NC_CACHE[key] = build(len(layers), final_unscaled)
    nc = _NC_CACHE[key]
    w = _prep_weights(inp, layers)
    aux = _aux_tables()
    xs_ = _shard_x(x2d)
    in_maps = []
    for core in range(NCORES):
        m = dict(w)
        m["xin"] = xs_[core]
        m["aux"] = aux[core]
        m["ident"] = np.eye(P, dtype=np.float32)
        in_maps.append(m)
    res = run_bass_kernel_spmd(nc, in_maps, core_ids=list(range(NCORES)))
    return _unshard(res.results)


def kernel(**inputs):
    x = np.asarray(inputs["x"], dtype=np.float32)
    x2d = x.reshape(SEQ, D)
    if FUSED:
        out = _launch(x2d, inputs, list(range(DEPTH)))
    else:
        out = x2d
        for l in range(DEPTH):
            out = _launch(out, inputs, [l])
    return out.reshape(1, SEQ, D).astype(np.float32)
```

```python
import numpy as np
import concourse.bass as bass
import concourse.mybir as mybir
from concourse.bass_utils import run_bass_kernel_spmd

F32 = mybir.dt.float32
BF16 = mybir.dt.bfloat16
AF = mybir.ActivationFunctionType
ALU = mybir.AluOpType

P = 128
DEPTH = 4
D = 1024
DFF = 2816
NJ = DFF // P
NKC = D // P
SEQ = 16384
NCORES = 8
NSTREAM = 2
OWN = SEQ // (NCORES * NSTREAM)
HALO = 60
T = OWN + 2 * HALO
TILES = [(0, 384), (384, 384), (768, 376)]
PAD = 16
TP = T + 2 * PAD
ALPHA = (2.0 * DEPTH) ** 0.25
EPS = 1e-5
NV = 161
NSF = 5
NSM = 3
SLOT = 1024
GROUPS = [(0, 8), (8, 7), (15, 7)]
GMAX = 8
BG_BOOST = 1.0
WIN_PERM = [14, 11, 15, 12, 16, 13, 0, 1, 5, 8, 2, 6, 9, 3, 7, 10, 4]
V_LN = [(0, 8), (16, 24), (32, 40)]
V_PSCALE = 48
V_SCONV = 50
V_CCONV = 59
V_CB = 152
V_CNG = 155
V_CNB = 158
EL = HALO - 16
ER = HALO + OWN - 16

INTERLEAVE = True


def tiles_for(h):
    lo = HALO - h
    W = OWN + 2 * h
    units = W // 2
    base, rem = divmod(units, 3)
    out = []
    c = lo
    for i in range(3):
        n = 2 * (base + (1 if i < rem else 0))
        out.append((c, n))
        c += n
    return out


def halo_even(h):
    return h + (h % 2)
FUSED = True


class Ev:
    __slots__ = ("sem", "val")

    def __init__(self, sem, val):
        self.sem = sem
        self.val = val


class Slot:
    __slots__ = ("ap", "free")

    def __init__(self, ap):
        self.ap = ap
        self.free = None


class Rot:
    def __init__(self, slots):
        self.slots = slots
        self.i = 0

    def next(self):
        s = self.slots[self.i % len(self.slots)]
        self.i += 1
        return s


class Prog:
    ENGS = ("pe", "act", "dve", "pool", "sp")

    def __init__(self, nc):
        self.nc = nc
        self.ops = {e: [] for e in self.ENGS}
        self.sems = {}
        self.cnt = {}
        self.nsem = 0

    def sem(self, name):
        if name not in self.sems:
            self.sems[name] = None
            self.cnt[name] = 0
        return name

    def emit(self, eng, fn, waits=(), sem=None, inc=1, signal=True):
        ws = [(w.sem, w.val) for w in waits if w is not None]
        ev = None
        s = None
        if signal:
            s = sem if sem is not None else self.sem("e_" + eng)
            self.sem(s)
            self.cnt[s] += inc
            ev = Ev(s, self.cnt[s])
        self.ops[eng].append((fn, ws, s, inc))
        return ev

    def replay(self, eng, e):
        waited = {}
        for fn, ws, s, inc in self.ops[eng]:
            for (ws_, wv) in ws:
                if waited.get(ws_, 0) < wv:
                    e.wait_ge(self.sems[ws_], wv)
                    waited[ws_] = wv
            if fn is None:
                continue
            ins = fn(e)
            if s is not None:
                ins.then_inc(self.sems[s], inc)


def build(nl, final_unscaled=True):
    nc = bass.Bass("TRN2", target_bir_lowering=False)

    xin = nc.dram_tensor("xin", [NSTREAM, P, NKC * T], F32, kind="ExternalInput").ap()
    auxd = nc.dram_tensor("aux", [NSTREAM, P, 130], F32, kind="ExternalInput").ap()
    vecd = nc.dram_tensor("vec", [P, nl * NV], F32, kind="ExternalInput").ap()
    wgud = nc.dram_tensor("wgu", [nl * 2 * NJ, P, 2048], F32, kind="ExternalInput").ap()
    wdd = nc.dram_tensor("wd", [nl * 2 * NKC, P, DFF], F32, kind="ExternalInput").ap()
    wind = nc.dram_tensor("win", [nl * 17, P, 1024], F32, kind="ExternalInput").ap()
    woutd = nc.dram_tensor("wout", [nl * NKC, P, 1024], F32, kind="ExternalInput").ap()
    wpoold = nc.dram_tensor("wpool", [nl, P, 256], F32, kind="ExternalInput").ap()
    identd = nc.dram_tensor("ident", [P, P], F32, kind="ExternalInput").ap()
    outd = nc.dram_tensor("out", [NSTREAM, P, NKC * OWN], F32, kind="ExternalOutput").ap()

    A = nc.alloc_sbuf_tensor
    xs = [A(f"xs{s}", [P, NKC, T], F32) for s in range(NSTREAM)]
    xb = [A(f"xb{s}", [P, NKC, T], BF16) for s in range(NSTREAM)]
    HSL = 19
    HM = A("HM", [P, 8 * T + 11712], BF16)
    HF = A("HF", [P, GMAX * T], BF16)
    ringF_t = A("ringF", [P, NSF, SLOT], BF16)
    ringM_t = A("ringM", [P, NSM, SLOT], BF16)
    vecs = A("vecs", [P, nl * NV], F32)
    auxs = [A(f"aux{s}", [P, 130], F32) for s in range(NSTREAM)]
    sgF_t = A("sgF", [P, 2, 384], F32)
    sgB_t = A("sgB", [P, 2, 384], F32)
    sq_t = A("sq", [P, 4, 384], BF16)
    nt_t = A("nt", [P, 2, 384], F32)
    mean_sb = A("mean_sb", [P, T], F32)
    var_sb = A("var_sb", [P, T], F32)
    ones_t = A("ones", [P, P], BF16)
    cst = A("cst", [P, 4], F32)
    ident_f = A("ident_f", [P, P], F32)
    dg_t = A("dg", [P, 2, P], BF16)
    banks = [nc.alloc_psum_tensor(f"bank{i}", [P, 512], F32) for i in range(8)]

    def Yslot(i):
        return HM[:, i * T:(i + 1) * T]

    def Hslot(i):
        return HF[:, i * T:(i + 1) * T]

    TA = HM[:, 8 * T:8 * T + 11712].bitcast(F32)

    def vcol(li, off):
        c = li * NV + off
        return vecs[:, c:c + 1]

    def emit_all(counts_in):
        p = Prog(nc)
        counts = {}
        sgF = Rot([Slot(sgF_t[:, i, :]) for i in range(2)])
        sgB = Rot([Slot(sgB_t[:, i, :]) for i in range(2)])
        sqring = Rot([Slot(sq_t[:, i, :]) for i in range(4)])
        ntring = Rot([Slot(nt_t[:, i, :]) for i in range(2)])
        dgring = Rot([Slot(dg_t[:, i, :]) for i in range(2)])
        bs = [Slot(b) for b in banks]
        bG = Rot([bs[0], bs[1]])
        bU = Rot([bs[2], bs[3]])
        bDn = Rot([bs[0], bs[1], bs[2], bs[3]])
        bP0 = bs[4]
        bP1 = bs[5]
        bS = Rot([bs[4], bs[5]])
        bM = bs[6]
        bQ = bs[7]

        class Ring:
            def __init__(self, tens, ns, name):
                self.t, self.ns, self.name = tens, ns, name
                self.plan = []
                self.issued = 0
                self.consumed = 0
                self.rel = {}
                self.ev = {}

            def issue_upto(self, k):
                while self.issued <= k and self.issued < len(self.plan):
                    m = self.issued
                    src, n = self.plan[m]
                    sl = m % self.ns
                    waits = []
                    if m >= self.ns:
                        if (m - self.ns) not in self.rel:
                            break
                        waits.append(self.rel[m - self.ns])
                    dst = self.t[:, sl, 0:n]
                    self.ev[m] = p.emit(
                        "pool", (lambda e, dst=dst, src=src: e.dma_start(out=dst, in_=src)),
                        waits, sem=p.sem(f"{self.name}{sl}"), inc=16)
                    self.issued += 1

            def next(self):
                k = self.consumed
                self.issue_upto(k + self.ns - 1)
                assert self.issued > k, "ring: load not issuable (missing release)"
                self.consumed += 1
                return k, self.t[:, k % self.ns, :], self.ev[k]

            def release(self, k, ev):
                self.rel[k] = ev

        ringF = Ring(ringF_t, NSF, "rf")
        ringM = Ring(ringM_t, NSM, "rm")

        def ffn_loads(li, f):
            base = (li * 2 + f)
            out = []
            for (j0, nj) in GROUPS:
                for jj in range(nj):
                    row = wgud[base * NJ + j0 + jj]
                    out.append((row[:, 0:1024], 1024))
                    out.append((row[:, 1024:2048], 1024))
                for dc in range(NKC):
                    out.append((wdd[base * NKC + dc][:, j0 * P:(j0 + nj) * P], nj * P))
            return out

        def mix_loads(li):
            b = li * 17
            out = [(wind[b + i], 1024) for i in range(8)]
            out.append((wpoold[li], 256))
            out += [(wind[b + i], 1024) for i in range(8, 17)]
            for dc in range(NKC):
                out.append((woutd[li * NKC + dc], 1024))
            return out

        for li in range(nl):
            for S in range(NSTREAM):
                ringF.plan += ffn_loads(li, 0)
            for S in range(NSTREAM):
                ringF.plan += ffn_loads(li, 1)
            for S in range(NSTREAM):
                ringM.plan += mix_loads(li)

        def mm(out, lhsT, rhs, start, stop, waits=(), signal=False):
            return p.emit("pe", lambda e: e.matmul(out, lhsT, rhs, start=start, stop=stop), waits, signal=signal)

        def act(out, in_, func, scale=1.0, bias=0.0, waits=()):
            return p.emit("act", lambda e: e.activation(out=out, in_=in_, func=func, bias=bias, scale=scale), waits)

        def tt(out, in0, in1, op, waits=()):
            return p.emit("dve", lambda e: e.tensor_tensor(out=out, in0=in0, in1=in1, op=op), waits)

        def stt(out, in0, scalar, in1, op0, op1, waits=()):
            return p.emit("dve", lambda e: e.scalar_tensor_tensor(out=out, in0=in0, scalar=scalar, in1=in1, op0=op0, op1=op1), waits)

        def ts(out, in0, s1, s2, op0, op1, waits=()):
            return p.emit("dve", lambda e: e.tensor_scalar(out=out, in0=in0, scalar1=s1, scalar2=s2, op0=op0, op1=op1), waits)

        def memset(out, val, waits=()):
            return p.emit("dve", lambda e: e.memset(out, val), waits)

        xb_ev = [[None] * 3 for _ in range(NSTREAM)]
        xs_ev = [[None] * 3 for _ in range(NSTREAM)]
        st = {"HF_free": [], "HM_free": [], "ln_free": None, "xs_all0": None, "xs_all1": None}

        ld_ev = []
        for s in range(NSTREAM):
            ld_ev.append(p.emit("sp", lambda e, s=s: e.dma_start(out=xs[s][:].rearrange("p a t -> p (a t)"), in_=xin[s]),
                                sem=p.sem(f"ldx{s}"), inc=16))
        aux_ev = [p.emit("sp", lambda e, s=s: e.dma_start(out=auxs[s][:], in_=auxd[s]), sem=p.sem(f"lda{s}"), inc=16)
                  for s in range(NSTREAM)]
        vec_ev = p.emit("sp", lambda e: e.dma_start(out=vecs[:], in_=vecd), sem=p.sem("ldv"), inc=16)
        ident_ev = p.emit("sp", lambda e: e.dma_start(out=ident_f[:], in_=identd), sem=p.sem("ldi"), inc=16)
        memset(ones_t[:], 1.0)
        memset(cst[0:64, 0:1], 0.5)
        memset(cst[64:128, 0:1], 0.25)
        memset(cst[0:64, 1:2], 0.125)
        const_ev = memset(cst[64:128, 1:2], 0.0625)
        for s in range(NSTREAM):
            e1 = None
            for dc in range(NKC):
                e1 = act(xb[s][:, dc, :], xs[s][:, dc, :], AF.Copy, waits=[ld_ev[s]])
            for ti in range(3):
                xb_ev[s][ti] = e1
                xs_ev[s][ti] = ld_ev[s]

        def ln_task(nch, inv_n, src, in_ev, out_fn, tiles):
            first = True
            d1 = a1 = None
            for ti, (c0, n) in enumerate(tiles):
                pe_ev = None

                def stat_mm(c, r1, r2, e1, e2, n=n):
                    m1 = mm(bM.ap[:, :n], ones_t[:], r1.ap[:, :n], c == 0, c == nch - 1,
                            [e1] + ([bM.free] if c == 0 else []), signal=True)
                    m2 = mm(bQ.ap[:, :n], ones_t[:], r2.ap[:, :n], c == 0, c == nch - 1,
                            [e2] + ([bQ.free] if c == 0 else []), signal=True)
                    r1.free = m1
                    r2.free = m2
                    return m2

                pend = None
                for c in range(nch):
                    r1 = sqring.next()
                    w = [in_ev[ti], r1.free]
                    if first:
                        w += [const_ev, vec_ev]
                    e1 = p.emit("dve", lambda e, o_=r1.ap[:, :n], i_=src(c, c0, n): e.tensor_copy(out=o_, in_=i_), w)
                    r2 = sqring.next()
                    e2 = act(r2.ap[:, :n], src(c, c0, n), AF.Square, waits=[in_ev[ti], r2.free])
                    if pend is not None:
                        pe_ev = stat_mm(*pend)
                    pend = (c, r1, r2, e1, e2)
                    first = False
                    yield 0.7
                pe_ev = stat_mm(*pend)
                w = [pe_ev]
                if ti == 0 and st["ln_free"] is not None:
                    w.append(st["ln_free"])
                a1 = act(mean_sb[:, c0:c0 + n], bM.ap[:, :n], AF.Identity, scale=inv_n, waits=w)
                nt = ntring.next()
                a2 = act(nt.ap[:, :n], bM.ap[:, :n], AF.Square, scale=inv_n, waits=[nt.free])
                bM.free = a2
                d1 = stt(var_sb[:, c0:c0 + n], bQ.ap[:, :n], inv_n, nt.ap[:, :n], ALU.mult, ALU.subtract,
                         [a2, pe_ev] + ([st["ln_free"]] if ti == 0 else []))
                bQ.free = d1
                nt.free = d1
                yield 1.2
            d2 = ts(var_sb[:], var_sb[:], 0.0, EPS, ALU.max, ALU.add, [d1])
            a3 = act(var_sb[:], var_sb[:], AF.Sqrt, waits=[d2])
            d3 = p.emit("dve", lambda e: e.reciprocal(out=var_sb[:], in_=var_sb[:]), [a3])
            d4 = stt(mean_sb[:], mean_sb[:], -1.0, var_sb[:], ALU.mult, ALU.mult, [d3, a1])
            yield 5.0
            last = None
            pend3 = None
            for ti, (c0, n) in enumerate(tiles):
                for c in range(nch):
                    nt = ntring.next()
                    e1 = tt(nt.ap[:, :n], src(c, c0, n), var_sb[:, c0:c0 + n], ALU.mult, [d4, nt.free])
                    e2 = tt(nt.ap[:, :n], nt.ap[:, :n], mean_sb[:, c0:c0 + n], ALU.add, [e1])
                    last = e2
                    if pend3 is not None:
                        pc, pti, pc0, pn, pnt, pe2 = pend3
                        pnt.free = out_fn(pc, pti, pc0, pn, pnt.ap[:, :pn], pe2)
                    pend3 = (c, ti, c0, n, nt, e2)
                    yield 1.3
            pc, pti, pc0, pn, pnt, pe2 = pend3
            pnt.free = out_fn(pc, pti, pc0, pn, pnt.ap[:, :pn], pe2)
            st["ln_free"] = last

        def main_ln(S, li, which, y_ev, tiles):
            goff, boff = V_LN[which]

            def src(c, c0, n):
                return xs[S][:, c, c0:c0 + n]

            def out_fn(c, ti, c0, n, nt_ap, ev):
                e1 = act(xs[S][:, c, c0:c0 + n], nt_ap, AF.Identity, scale=vcol(li, goff + c), bias=vcol(li, boff + c), waits=[ev])
                e2 = act(xb[S][:, c, c0:c0 + n], nt_ap, AF.Identity, scale=vcol(li, goff + c), bias=vcol(li, boff + c), waits=[ev])
                xs_ev[S][ti] = e1
                xb_ev[S][ti] = e2
                st["xs_all%d" % S] = e1
                return e2

            return ln_task(NKC, 1.0 / D, src, y_ev, out_fn, tiles)

        def h_sync(key):
            if st[key]:
                p.emit("act", None, st[key], signal=False)
                p.emit("dve", None, st[key], signal=False)
            st[key] = []

        def ffn_phase(S, li, f, y_ev, TILES):
            X = xb[S]
            XS = xs[S]
            h_sync("HF_free")
            hfree = None
            last_down = None
            h_last_dve = None
            for gi, (j0, nj) in enumerate(GROUPS):
                h_ev = [[None] * 3 for _ in range(nj)]
                pend_h = None

                def h_mul(jj, ti, c0, n, sgs, u_b, a_ev, up_ev, h_ev=h_ev, hfree=hfree):
                    d_ev = stt(Hslot(jj)[:, c0:c0 + n], sgs.ap[:, :n], 0.5, u_b.ap[:, :n], ALU.mult, ALU.mult, [a_ev, up_ev, hfree])
                    u_b.free = d_ev
                    sgs.free = d_ev
                    h_ev[jj][ti] = d_ev
                    return d_ev

                for jj in range(nj):
                    kg, slot_g, wev_g = ringF.next()
                    ku, slot_u, wev_u = ringF.next()
                    wg = slot_g.rearrange("p (k m) -> p k m", k=NKC)
                    wu = slot_u.rearrange("p (k m) -> p k m", k=NKC)
                    up_ev = gate_ev = None
                    for ti, (c0, n) in enumerate(TILES):
                        g_b = bG.next()
                        u_b = bU.next()
                        for kc in range(NKC):
                            w = [wev_g, xb_ev[S][ti], g_b.free] if kc == 0 else []
                            gate_ev = mm(g_b.ap[:, :n], wg[:, kc, :], X[:, kc, c0:c0 + n], kc == 0, kc == NKC - 1, w, signal=(kc == NKC - 1))
                        for kc in range(NKC):
                            w = [wev_u, u_b.free] if kc == 0 else []
                            up_ev = mm(u_b.ap[:, :n], wu[:, kc, :], X[:, kc, c0:c0 + n], kc == 0, kc == NKC - 1, w, signal=(kc == NKC - 1))
                        sgs = sgF.next()
                        a_ev = act(sgs.ap[:, :n], g_b.ap[:, :n], AF.Silu, waits=[gate_ev, sgs.free])
                        g_b.free = a_ev
                        if pend_h is not None:
                            h_last_dve = h_mul(*pend_h)
                        pend_h = (jj, ti, c0, n, sgs, u_b, a_ev, up_ev)
                        yield 2.9
                    ringF.release(kg, gate_ev)
                    ringF.release(ku, up_ev)
                h_last_dve = h_mul(*pend_h)
                pend_h = None
                for dc in range(NKC):
                    k, slot, wev = ringF.next()
                    ev = None
                    for ti, (c0, n) in enumerate(TILES):
                        d_b = bDn.next()
                        for kk in range(nj):
                            w = [wev, d_b.free, h_ev[nj - 1][ti]] if kk == 0 else []
                            ev = mm(d_b.ap[:, :n], slot[:, kk * P:(kk + 1) * P], Hslot(kk)[:, c0:c0 + n], kk == 0, kk == nj - 1, w, signal=(kk == nj - 1))
                        if gi == 0:
                            d_ev = stt(XS[:, dc, c0:c0 + n], XS[:, dc, c0:c0 + n], ALPHA, d_b.ap[:, :n], ALU.mult, ALU.add, [ev, xs_ev[S][ti]])
                        else:
                            d_ev = tt(XS[:, dc, c0:c0 + n], XS[:, dc, c0:c0 + n], d_b.ap[:, :n], ALU.add, [ev, y_ev[ti]])
                        d_b.free = d_ev
                        if dc == NKC - 1:
                            y_ev[ti] = d_ev
                        yield 0.18 * nj + 0.1
                    ringF.release(k, ev)
                    last_down = ev
                hfree = last_down
            st["HF_free"] = [last_down, h_last_dve]

        def mixer_phase(S, li, y_ev, TILES_A, TILES_B):
            X = xb[S]
            XS = xs[S]
            AX = auxs[S]
            mL = AX[:, 0:1]
            mR = AX[:, 1:2]
            h_sync("HM_free")
            Y = [Yslot(i) for i in range(8)]

            def wview(slot):
                return slot.rearrange("p (k m) -> p k m", k=NKC)

            def inproj(bank, wv_c, ti, c0, n, extra):
                ev = None
                for kc in range(NKC):
                    w = ([xb_ev[S][ti], bank.free] + extra) if kc == 0 else []
                    ev = mm(bank.ap[:, :n], wv_c[:, kc, :], X[:, kc, c0:c0 + n], kc == 0, kc == NKC - 1, w, signal=(kc == NKC - 1))
                return ev

            ABF = HM[:, 8 * T:8 * T + 3 * TP]
            abuf = [ABF[:, c * TP:(c + 1) * TP] for c in range(3)]
            CO0 = (3 * TP) // 2
            cobuf = [TA[:, CO0 + c * T:CO0 + (c + 1) * T] for c in range(3)]
            a3 = ABF.rearrange("p (c t) -> p c t", c=3)
            memset(a3[:, :, 0:PAD], 0.0)
            pad_ev = memset(a3[:, :, TP - PAD:TP], 0.0)
            a_done = [None] * 3
            for c in range(3):
                k1, slot1, wev1 = ringM.next()
                k2, slot2, wev2 = ringM.next()
                ge = ve = d_ev = None
                for ti, (c0, n) in enumerate(TILES_A):
                    ge = inproj(bP0, wview(slot1), ti, c0, n, [wev1])
                    ve = inproj(bP1, wview(slot2), ti, c0, n, [wev2])
                    sgs = sgB.next()
                    a_ev = act(sgs.ap[:, :n], bP0.ap[:, :n], AF.Tanh, scale=0.5, waits=[ge, sgs.free])
                    bP0.free = a_ev
                    d_ev = stt(abuf[c][:, PAD + c0:PAD + c0 + n], sgs.ap[:, :n], 1.0, bP1.ap[:, :n], ALU.add, ALU.mult, [a_ev, ve, pad_ev])
                    bP1.free = d_ev
                    sgs.free = d_ev
                    yield 2.9
                ringM.release(k1, ge)
                ringM.release(k2, ve)
                ts(abuf[c][:, PAD:PAD + HALO], abuf[c][:, PAD:PAD + HALO], mL, 0.0, ALU.mult, ALU.add, [d_ev, aux_ev[S]])
                a_done[c] = ts(abuf[c][:, PAD + HALO + OWN:PAD + T], abuf[c][:, PAD + HALO + OWN:PAD + T], mR, 0.0, ALU.mult, ALU.add, [d_ev])
            cbanks = [bs[4], bs[5], bs[6]]
            co_ev = [None] * 3
            for c in range(3):
                last_me = [None] * 3

                def build_diag(kk, c=c):
                    dg_ = dgring.next()
                    out_ap, w_ap = dg_.ap, vcol(li, V_CCONV + kk * 3 + c)
                    de_ = p.emit("pool", lambda e: e.tensor_scalar(out=out_ap, in0=ident_f[:], scalar1=w_ap, scalar2=0.0,
                                                                    op0=ALU.mult, op1=ALU.add),
                                 [dg_.free, vec_ev, ident_ev])
                    return dg_, de_

                nxt = build_diag(0)
                for kk in range(31):
                    dg, de = nxt
                    if kk + 1 < 31:
                        nxt = build_diag(kk + 1)
                    me = None
                    for ti, (c0, n) in enumerate(TILES_B):
                        w = [de] + ([a_done[c], cbanks[ti].free] if kk == 0 else [])
                        me = mm(cbanks[ti].ap[:, :n], dg.ap, abuf[c][:, PAD + c0 + kk - 15:PAD + c0 + kk - 15 + n], kk == 0, kk == 30, w,
                                signal=(kk == 30 or ti == 2))
                        if kk == 30:
                            last_me[ti] = me
                    dg.free = me
                    yield 0.6
                for ti, (c0, n) in enumerate(TILES_B):
                    ae = act(cobuf[c][:, c0:c0 + n], cbanks[ti].ap[:, :n], AF.Identity, scale=0.5, bias=vcol(li, V_CB + c), waits=[last_me[ti]])
                    cbanks[ti].free = ae
                    co_ev[ti] = ae
                yield 1.5
            yc_ev = [None] * 3

            def c_src(c, c0, n):
                return cobuf[c][:, c0:c0 + n]

            def c_out(c, ti, c0, n, nt_ap, ev):
                e = act(Y[5 + c][:, c0:c0 + n], nt_ap, AF.Silu, scale=vcol(li, V_CNG + c), bias=vcol(li, V_CNB + c), waits=[ev])
                yc_ev[ti] = e
                return e

            for cst_ in ln_task(3, 1.0 / 384.0, c_src, co_ev, c_out, TILES_B):
                yield cst_
            c_done_dve = st["ln_free"]
            c_done_act = yc_ev[2]

            ub = [TA[:, c * TP:(c + 1) * TP] for c in range(2)]
            s_a = TA[:, 2 * TP:3 * TP]
            s_b = TA[:, 3 * TP:4 * TP]
            tot = TA[:, 4 * TP:4 * TP + T]
            u2 = TA[:, 0:2 * TP].rearrange("p (c t) -> p c t", c=2)
            p.emit("act", None, [c_done_dve], signal=False)
            p.emit("dve", None, [c_done_act], signal=False)
            memset(u2[:, :, 0:PAD], 0.0)
            pad_ev = memset(u2[:, :, TP - PAD:TP], 0.0)
            u_done = [None] * 2
            for c in range(2):
                k, slot, wev = ringM.next()
                a_ev = ue = None
                for ti, (c0, n) in enumerate(TILES_A):
                    d_b = bS.next()
                    ue = inproj(d_b, wview(slot), ti, c0, n, [wev])
                    a_ev = act(ub[c][:, PAD + c0:PAD + c0 + n], d_b.ap[:, :n], AF.Copy, waits=[ue, pad_ev, c_done_dve])
                    d_b.free = a_ev
                    yield 1.5
                ringM.release(k, ue)
                ts(ub[c][:, PAD:PAD + HALO], ub[c][:, PAD:PAD + HALO], mL, 0.0, ALU.mult, ALU.add, [a_ev, aux_ev[S]])
                u_done[c] = ts(ub[c][:, PAD + HALO + OWN:PAD + T], ub[c][:, PAD + HALO + OWN:PAD + T], mR, 0.0, ALU.mult, ALU.add, [a_ev])
            k, slot, wev = ringM.next()
            pw = slot[:, 0:256].rearrange("p (c m) -> p c m", c=2)
            pooled = [Y[2], Y[3]]
            pool_dve = None
            ya_ev = None
            pool_ready = [None, None]
            for c in range(2):
                ev = u_done[c]
                for hf in range(2):
                    r0, r1 = hf * 64, hf * 64 + 64
                    gidx = 2 * c + hf
                    w_ = 2 ** (gidx + 1)
                    srcb = ub[c]
                    bufs = [s_a, s_b]
                    step = 1
                    nb = 0
                    while step * 2 < w_:
                        dst = bufs[nb % 2]
                        L = TP - step
                        ev = tt(dst[r0:r1, 0:L], srcb[r0:r1, 0:L], srcb[r0:r1, step:step + L], ALU.add, [ev])
                        srcb = dst
                        nb += 1
                        step *= 2
                    h2 = w_ // 2
                    ev = tt(tot[r0:r1, :], srcb[r0:r1, PAD - h2:PAD - h2 + T], srcb[r0:r1, PAD:PAD + T], ALU.add, [ev])
                ev = stt(pooled[c][:, :], tot[:, :], cst[:, c:c + 1], ub[c][:, PAD:PAD + T], ALU.mult, ALU.subtract, [ev, const_ev])
                for e_i, e0 in enumerate((EL, ER)):
                    nt = ntring.next()
                    e1 = tt(nt.ap[:, 0:32], tot[:, e0:e0 + 32], AX[:, 2 + c * 64 + e_i * 32:2 + c * 64 + e_i * 32 + 32], ALU.mult, [ev, nt.free])
                    ev = tt(pooled[c][:, e0:e0 + 32], nt.ap[:, 0:32], ub[c][:, PAD + e0:PAD + e0 + 32], ALU.subtract, [e1])
                    nt.free = ev
                yield 8.0
                pool_ready[c] = ev
                pool_dve = ev
            pool_k, pool_wev = k, wev

            def pool_matmuls():
                me = ya = None
                for c in range(2):
                    for ti, (c0, n) in enumerate(TILES_B):
                        d_b = bS.next()
                        me = mm(d_b.ap[:, :n], pw[:, c, :], pooled[c][:, c0:c0 + n], True, True, [pool_wev, pool_ready[c], d_b.free], signal=True)
                        ya = act(Y[c][:, c0:c0 + n], d_b.ap[:, :n], AF.Identity, scale=vcol(li, V_PSCALE + c), waits=[me])
                        d_b.free = ya
                ringM.release(pool_k, me)
                return me, ya

            gcv = [TA[:, c * TP:(c + 1) * TP] for c in range(3)]
            cv = TA[:, 3 * TP:3 * TP + T]
            g3 = TA[:, 0:3 * TP].rearrange("p (c t) -> p c t", c=3)
            memset(g3[:, :, 0:PAD], 0.0, [pool_dve])
            pad_ev = memset(g3[:, :, TP - PAD:TP], 0.0)
            cv_free = None
            yb_last = None
            for c in range(3):
                k1, slot1, wev1 = ringM.next()
                k2, slot2, wev2 = ringM.next()
                ge = ve = d_ev = None
                for ti, (c0, n) in enumerate(TILES_A):
                    ge = inproj(bP0, wview(slot1), ti, c0, n, [wev1])
                    ve = inproj(bP1, wview(slot2), ti, c0, n, [wev2])
                    sgs = sgB.next()
                    a_ev = act(sgs.ap[:, :n], bP0.ap[:, :n], AF.Copy, waits=[ge, sgs.free])
                    bP0.free = a_ev
                    d_ev = tt(gcv[c][:, PAD + c0:PAD + c0 + n], sgs.ap[:, :n], bP1.ap[:, :n], ALU.mult, [a_ev, ve, pad_ev])
                    bP1.free = d_ev
                    sgs.free = d_ev
                    yield 2.9
                ringM.release(k1, ge)
                ringM.release(k2, ve)
                ts(gcv[c][:, PAD:PAD + HALO], gcv[c][:, PAD:PAD + HALO], mL, 0.0, ALU.mult, ALU.add, [d_ev])
                ev = ts(gcv[c][:, PAD + HALO + OWN:PAD + T], gcv[c][:, PAD + HALO + OWN:PAD + T], mR, 0.0, ALU.mult, ALU.add, [d_ev])
                ev = ts(cv, gcv[c][:, PAD - 1:PAD - 1 + T], vcol(li, V_SCONV + c), 0.0, ALU.mult, ALU.add, [ev, cv_free])
                ev = stt(cv, gcv[c][:, PAD:PAD + T], vcol(li, V_SCONV + 3 + c), cv, ALU.mult, ALU.add, [ev])
                ev = stt(cv, gcv[c][:, PAD + 1:PAD + 1 + T], vcol(li, V_SCONV + 6 + c), cv, ALU.mult, ALU.add, [ev])
                if c == 0:
                    pool_pe, ya_ev = pool_matmuls()
                    p.emit("dve", None, [pool_pe], signal=False)
                    yield 2.0
                k, slot, wev = ringM.next()
                be = None
                for ti, (c0, n) in enumerate(TILES_B):
                    d_b = bS.next()
                    be = inproj(d_b, wview(slot), ti, c0, n, [wev])
                    d2 = tt(Y[2 + c][:, c0:c0 + n], d_b.ap[:, :n], cv[:, c0:c0 + n], ALU.mult, [be, ev])
                    d_b.free = d2
                    cv_free = d2
                    yb_last = d2
                    yield 3.0
                ringM.release(k, be)

            ev = None
            for dc in range(NKC):
                k, slot, wev = ringM.next()
                wo = wview(slot)
                for ti, (c0, n) in enumerate(TILES_B):
                    d_b = bS.next()
                    for kc in range(NKC):
                        w = [wev, d_b.free, yb_last, yc_ev[2], ya_ev] if kc == 0 else []
                        ev = mm(d_b.ap[:, :n], wo[:, kc, :], Y[kc][:, c0:c0 + n], kc == 0, kc == NKC - 1, w, signal=(kc == NKC - 1))
                    d_ev = stt(XS[:, dc, c0:c0 + n], XS[:, dc, c0:c0 + n], ALPHA, d_b.ap[:, :n], ALU.mult, ALU.add, [ev, st["xs_all%d" % S]])
                    d_b.free = d_ev
                    if dc == NKC - 1:
                        y_ev[ti] = d_ev
                    yield 1.9
                ringM.release(k, ev)
            st["HM_free"] = [ev, yb_last]

        def counted(key, gen):
            tot = 0.0
            for c in gen:
                tot += c
                yield c
            counts[key] = tot

        def chain(facts):
            for key, f in facts:
                yield from counted(key, f())

        def run(fkey, fg, bg_facts):
            fg = counted(fkey, fg)
            bg = chain(bg_facts) if bg_facts else None
            nfg = counts_in.get(fkey, 0.0) if counts_in else 0.0
            nbg = sum(counts_in.get(k, 0.0) for k, _ in bg_facts) if counts_in else 0.0
            ratio = (nbg / nfg * BG_BOOST) if nfg else 1.0
            fg_acc = 0.0
            bg_acc = 0.0
            alive = bg is not None
            for c in fg:
                fg_acc += c
                while alive and bg_acc < fg_acc * ratio:
                    try:
                        bg_acc += next(bg)
                    except StopIteration:
                        alive = False
            if alive:
                for _ in bg:
                    pass

        yev = {}

        def Y_(S, li, tag):
            return yev.setdefault((S, li, tag), [None] * 3)

        def hA(li):
            return halo_even(min(HALO, 15 * (nl - li)))

        def hB(li):
            return halo_even(min(HALO, 15 * (nl - li - 1)))

        def f_ffn(S, li, f):
            tag = "f1" if f == 0 else "f2"
            return ffn_phase(S, li, f, Y_(S, li, tag), tiles_for(hA(li) if f == 0 else hB(li)))

        def b_ln(S, li, which):
            tag = ("f1", "mix", "f2")[which]
            tl = tiles_for(hA(li) if which == 0 else hB(li))
            return (("ln", S, li, which), lambda: main_ln(S, li, which, Y_(S, li, tag), tl))

        def b_mix(S, li):
            return (("mix", S, li), lambda: mixer_phase(S, li, Y_(S, li, "mix"), tiles_for(hA(li)), tiles_for(hB(li))))

        for li in range(nl):
            run(("f", 0, li, 0), f_ffn(0, li, 0), [b_ln(1, li - 1, 2)] if li > 0 else [])
            run(("f", 1, li, 0), f_ffn(1, li, 0), [b_ln(0, li, 0), b_mix(0, li), b_ln(0, li, 1)])
            run(("f", 0, li, 1), f_ffn(0, li, 1), [b_ln(1, li, 0), b_mix(1, li), b_ln(1, li, 1)])
            run(("f", 1, li, 1), f_ffn(1, li, 1), [b_ln(0, li, 2)])
        for _ in chain([b_ln(1, nl - 1, 2)]):
            pass

        st_ev = []
        for s in range(NSTREAM):
            st_ev.append(p.emit("sp", lambda e, s=s: e.dma_start(
                out=outd[s].rearrange("p (a t) -> p a t", a=NKC), in_=xs[s][:, :, HALO:HALO + OWN]),
                [xs_ev[s][0], xs_ev[s][1], xs_ev[s][2]], sem=p.sem(f"st{s}"), inc=16))
        p.emit("sp", None, st_ev, signal=False)
        return p, counts

    _, counts0 = emit_all(None)
    p, _ = emit_all(counts0)

    from contextlib import ExitStack
    with ExitStack() as stack:
        for name in list(p.sems.keys()):
            p.sems[name] = stack.enter_context(nc.semaphore(name))
        block = stack.enter_context(nc.Block())

        @block.tensor
        def _(e):
            p.replay("pe", e)

        @block.scalar
        def _(e):
            p.replay("act", e)

        @block.vector
        def _(e):
            p.replay("dve", e)

        @block.gpsimd
        def _(e):
            p.replay("pool", e)

        @block.sync
        def _(e):
            p.replay("sp", e)
    return nc


def _prep_weights(inp, layers):
    nl = len(layers)
    f32 = np.float32

    def blk(w):
        K, M = w.shape
        return w.reshape(K // P, P, M).transpose(1, 0, 2)

    wgu = np.empty((nl * 2 * NJ, P, 2, NKC, P), f32)
    wd = np.empty((nl * 2 * NKC, P, NJ, P), f32)
    win = np.empty((nl * 17, P, NKC, P), f32)
    wout = np.empty((nl * NKC, P, NKC, P), f32)
    wpool = np.zeros((nl, P, 2, P), f32)
    vec = np.zeros((P, nl * NV), f32)
    for i, l in enumerate(layers):
        for f, (kg, ku, kd) in enumerate((("ffn1_w_gate", "ffn1_w_up", "ffn1_w_down"),
                                          ("ffn2_w_gate", "ffn2_w_up", "ffn2_w_down"))):
            g = blk(np.asarray(inp[kg][l]))
            u = blk(np.asarray(inp[ku][l]))
            d = blk(np.asarray(inp[kd][l]))
            base = (i * 2 + f)
            for j in range(NJ):
                wgu[base * NJ + j, :, 0] = g[:, :, j * P:(j + 1) * P]
                wgu[base * NJ + j, :, 1] = u[:, :, j * P:(j + 1) * P]
            for dc in range(NKC):
                wd[base * NKC + dc] = d[:, :, dc * P:(dc + 1) * P]
        wi = blk(np.asarray(inp["mix_w_in"][l]))
        for n_, c in enumerate(WIN_PERM):
            win[i * 17 + n_] = wi[:, :, c * P:(c + 1) * P]
        wo = blk(np.asarray(inp["mix_w_out"][l]))
        for dc in range(NKC):
            wout[i * NKC + dc] = wo[:, :, dc * P:(dc + 1) * P]
        pw = np.asarray(inp["pool_w"][l])
        for c in range(2):
            for hf in range(2):
                wpool[i, hf * 64:(hf + 1) * 64, c, hf * 64:(hf + 1) * 64] = pw[2 * c + hf]
        o = i * NV

        def col(v):
            v = np.asarray(v)
            return v.reshape(-1, P).T

        for w_, (kg, kb) in enumerate((("ln1_g", "ln1_b"), ("ln2_g", "ln2_b"), ("ln3_g", "ln3_b"))):
            vec[:, o + V_LN[w_][0]:o + V_LN[w_][0] + 8] = col(inp[kg][l])
            vec[:, o + V_LN[w_][1]:o + V_LN[w_][1] + 8] = col(inp[kb][l])
        vec[:, o + V_PSCALE:o + V_PSCALE + 2] = col(inp["pool_scale"][l])
        sc = np.asarray(inp["sconv_w"][l])
        for k in range(3):
            vec[:, o + V_SCONV + k * 3:o + V_SCONV + k * 3 + 3] = col(sc[k])
        cc = np.asarray(inp["cconv_w"][l])
        for k in range(31):
            vec[:, o + V_CCONV + k * 3:o + V_CCONV + k * 3 + 3] = col(cc[k])
        vec[:, o + V_CB:o + V_CB + 3] = col(inp["cconv_b"][l])
        vec[:, o + V_CNG:o + V_CNG + 3] = col(inp["cnorm_g"][l])
        vec[:, o + V_CNB:o + V_CNB + 3] = col(inp["cnorm_b"][l])
    return dict(
        wgu=wgu.reshape(nl * 2 * NJ, P, 2048), wd=wd.reshape(nl * 2 * NKC, P, DFF),
        win=win.reshape(nl * 17, P, 1024), wout=wout.reshape(nl * NKC, P, 1024),
        wpool=wpool.reshape(nl, P, 256), vec=vec)


def _aux_tables():
    aux = np.ones((NCORES, NSTREAM, P, 130), np.float32)
    for core in range(NCORES):
        for s in range(NSTREAM):
            q = core * NSTREAM + s
            aux[core, s, :, 0] = 0.0 if q == 0 else 1.0
            aux[core, s, :, 1] = 0.0 if q == NCORES * NSTREAM - 1 else 1.0
            for c in range(2):
                for hf in range(2):
                    w = 2 ** (2 * c + hf + 1)
                    for e_i, e0 in enumerate((EL, ER)):
                        for i in range(32):
                            t = q * OWN - HALO + e0 + i
                            lo = min(max(t - w // 2, 0), SEQ)
                            hi = min(max(t - w // 2 + w, 0), SEQ)
                            cnt = max(hi - lo, 1)
                            aux[core, s, hf * 64:(hf + 1) * 64, 2 + c * 64 + e_i * 32 + i] = 1.0 / cnt
    return aux


def _shard_x(x2d):
    xp = np.zeros((SEQ + 2 * HALO, D), np.float32)
    xp[HALO:HALO + SEQ] = x2d
    outs = []
    for core in range(NCORES):
        arr = np.empty((NSTREAM, P, NKC, T), np.float32)
        for s in range(NSTREAM):
            q = core * NSTREAM + s
            seg = xp[q * OWN:q * OWN + T]
            arr[s] = seg.T.reshape(NKC, P, T).transpose(1, 0, 2)
        outs.append(arr.reshape(NSTREAM, P, NKC * T))
    return outs


def _unshard(res):
    out = np.empty((SEQ, D), np.float32)
    for core in range(NCORES):
        o = np.asarray(res[core]["out"]).reshape(NSTREAM, P, NKC, OWN)
        for s in range(NSTREAM):
            q = core * NSTREAM + s
            out[q * OWN:(q + 1) * OWN] = o[s].transpose(2, 1, 0).reshape(OWN, D)
    return out


_NC_CACHE = {}


def _launch(x2d, inp, layers, final_unscaled=True):
    key = (len(layers), final_unscaled)
    if key not in _NC_CACHE:
        _NC_CACHE[key] = build(len(layers), final_unscaled)
    nc = _NC_CACHE[key]
    w = _prep_weights(inp, layers)
    aux = _aux_tables()
    xs_ = _shard_x(x2d)
    in_maps = []
    for core in range(NCORES):
        m = dict(w)
        m["xin"] = xs_[core]
        m["aux"] = aux[core]
        m["ident"] = np.eye(P, dtype=np.float32)
        in_maps.append(m)
    res = run_bass_kernel_spmd(nc, in_maps, core_ids=list(range(NCORES)))
    return _unshard(res.results)


def kernel(**inputs):
    x = np.asarray(inputs["x"], dtype=np.float32)
    x2d = x.reshape(SEQ, D)
    if FUSED:
        out = _launch(x2d, inputs, list(range(DEPTH)))
    else:
        out = x2d
        for l in range(DEPTH):
            out = _launch(out, inputs, [l])
    return out.reshape(1, SEQ, D).astype(np.float32)
```
